# Optimizing a Trainium2 kernel written in Bass

```python
import math
import jax, jax.numpy as jnp
from jax import lax
import numpy as np

D_MODEL = 4096
BATCH = 4
SEQ = 2048
DEPTH = 2

N_MIXERS = 2
N_META = 16
CHUNK = 64
DK = 128
DV = 128
NK = D_MODEL // 128
NV = 2 * NK
KEY_DIM = NK * DK
VAL_DIM = NV * DV
QKV_DIM = 2 * KEY_DIM + VAL_DIM
IN_DIM = QKV_DIM + VAL_DIM + 2 * NV
CONV_K = 4
POOL_WINDOWS = (2, 4, 8, 16)
N_GROUPS = len(POOL_WINDOWS)
GROUP_W = D_MODEL // N_GROUPS
D_FF = 4 * D_MODEL
EPS = 1e-6

kernel_name = "hybrid_deltanet_pool_sqrelu"


def rmsnorm(x, g):
    x32 = x.astype(jnp.float32)
    y = x32 * lax.rsqrt(jnp.mean(x32 * x32, axis=-1, keepdims=True) + EPS)
    return (y * g.astype(jnp.float32)).astype(x.dtype)


def l2norm(x):
    return x * lax.rsqrt(jnp.sum(x * x, axis=-1, keepdims=True) + EPS)


def causal_depthwise_conv(x, w):
    c = x.shape[-1]
    kern = w.astype(x.dtype)[:, None, :]
    return lax.conv_general_dilated(x, kern, window_strides=(1,), padding=[(CONV_K - 1, 0)],
                                    dimension_numbers=("NWC", "WIO", "NWC"), feature_group_count=c)


def chunk_gated_delta_rule(q, k, v, g, beta):
    bsz, seq_len, heads, _ = q.shape
    pad = (-N_META) % CHUNK
    fp = lambda t: jnp.pad(t, [(0, 0), (pad, 0)] + [(0, 0)] * (t.ndim - 2))
    q, k, v, g, beta = (fp(t) for t in (q, k, v, g, beta))
    n = (seq_len + pad) // CHUNK

    def to_chunks(t):
        t = t.reshape((bsz, n, CHUNK, heads) + t.shape[3:])
        return jnp.moveaxis(t, 3, 1)

    q = to_chunks(q) * (DK ** -0.5)
    k, v, g, beta = (to_chunks(t) for t in (k, v, g, beta))
    k_beta = k * beta[..., None]
    v_beta = v * beta[..., None]
    gc = jnp.cumsum(g, axis=-1)

    lower = jnp.tril(jnp.ones((CHUNK, CHUNK), dtype=bool))
    strict = jnp.tril(jnp.ones((CHUNK, CHUNK), dtype=bool), -1)
    diff = gc[..., :, None] - gc[..., None, :]
    decay = jnp.where(lower, jnp.exp(jnp.where(lower, diff, 0.0)), 0.0)

    a_mat = jnp.where(strict, jnp.einsum("bhncd,bhnsd->bhncs", k_beta, k) * decay, 0.0)
    eye = jnp.eye(CHUNK, dtype=jnp.float32)
    t_mat = lax.linalg.triangular_solve(eye + a_mat, jnp.broadcast_to(eye, a_mat.shape),
                                        left_side=True, lower=True, unit_diagonal=True)
    u = jnp.einsum("bhncs,bhnse->bhnce", t_mat, v_beta)
    w = jnp.einsum("bhncs,bhnsd->bhncd", t_mat, k_beta * jnp.exp(gc)[..., None])

    qk = jnp.where(lower, jnp.einsum("bhncd,bhnsd->bhncs", q, k) * decay, 0.0)
    q_dec = q * jnp.exp(gc)[..., None]
    k_dec = k * jnp.exp(gc[..., -1:] - gc)[..., None]
    g_last = jnp.exp(gc[..., -1])

    xs = tuple(jnp.moveaxis(t, 2, 0) for t in (qk, q_dec, k_dec, u, w, g_last))

    def step(state, inp):
        qk_i, qd_i, kd_i, u_i, w_i, gl_i = inp
        v_new = u_i - jnp.einsum("bhcd,bhde->bhce", w_i, state)
        o_i = jnp.einsum("bhcd,bhde->bhce", qd_i, state) + jnp.einsum("bhcs,bhse->bhce", qk_i, v_new)
        state = state * gl_i[..., None, None] + jnp.einsum("bhcd,bhce->bhde", kd_i, v_new)
        return state, o_i

    s0 = jnp.zeros((bsz, heads, DK, DV), jnp.float32)
    _, o = lax.scan(step, s0, xs)
    o = jnp.moveaxis(jnp.moveaxis(o, 0, 2), 1, 3).reshape(bsz, n * CHUNK, heads, DV)
    return o[:, pad:]


def gated_deltanet(h, w_in, conv_w, a_log, dt_bias, out_norm, w_out):
    bsz, seq_len, _ = h.shape
    proj = h @ w_in.astype(h.dtype)
    qkv = proj[..., :QKV_DIM]
    z = proj[..., QKV_DIM:QKV_DIM + VAL_DIM]
    b = proj[..., QKV_DIM + VAL_DIM:QKV_DIM + VAL_DIM + NV]
    a = proj[..., QKV_DIM + VAL_DIM + NV:]
    qkv = jax.nn.silu(causal_depthwise_conv(qkv, conv_w)).astype(jnp.float32)
    q = l2norm(qkv[..., :KEY_DIM].reshape(bsz, seq_len, NK, DK))
    k = l2norm(qkv[..., KEY_DIM:2 * KEY_DIM].reshape(bsz, seq_len, NK, DK))
    v = qkv[..., 2 * KEY_DIM:].reshape(bsz, seq_len, NV, DV)
    q = jnp.repeat(q, NV // NK, axis=2)
    k = jnp.repeat(k, NV // NK, axis=2)
    beta = jax.nn.sigmoid(b.astype(jnp.float32))
    g = -jnp.exp(a_log.astype(jnp.float32)) * jax.nn.softplus(a.astype(jnp.float32) + dt_bias.astype(jnp.float32))
    o = chunk_gated_delta_rule(q, k, v, g, beta)
    o = o * lax.rsqrt(jnp.mean(o * o, axis=-1, keepdims=True) + EPS) * out_norm.astype(jnp.float32)
    o = o * jax.nn.silu(z.astype(jnp.float32).reshape(bsz, seq_len, NV, DV))
    return o.reshape(bsz, seq_len, VAL_DIM).astype(h.dtype) @ w_out.astype(h.dtype)


def multiscale_pool(h, w_pool, scale):
    bsz, seq_len, _ = h.shape
    hg = h.astype(jnp.float32).reshape(bsz, seq_len, N_GROUPS, GROUP_W)
    cs = jnp.cumsum(hg, axis=1)
    pos = jnp.arange(1, seq_len + 1, dtype=jnp.float32)
    means = []
    for gi, win in enumerate(POOL_WINDOWS):
        c = cs[:, :, gi]
        lag = jnp.pad(c, ((0, 0), (win, 0), (0, 0)))[:, :seq_len]
        means.append((c - lag) / jnp.minimum(pos, float(win))[None, :, None])
    pooled = (jnp.stack(means, axis=2) - hg).astype(h.dtype)
    y = jnp.einsum("blgc,gcd->blgd", pooled, w_pool.astype(h.dtype)).reshape(bsz, seq_len, D_MODEL)
    return y * scale.astype(h.dtype)


def sq_relu_mlp(h, w_up, w_down):
    u = h @ w_up.astype(h.dtype)
    r = jnp.maximum(u, 0)
    return (r * r) @ w_down.astype(h.dtype)


def setup_inputs(seed: int = 0) -> dict:
    key = jax.random.key(seed)
    ks = jax.random.split(key, 20)
    n_a = len(range(0, DEPTH, N_MIXERS))
    n_b = len(range(1, DEPTH, N_MIXERS))
    nrm = lambda k, s: jax.random.normal(k, s, jnp.float32)
    gain = lambda k, s: 1.0 + 0.02 * nrm(k, s)
    return {
        "x": nrm(ks[0], (BATCH, SEQ, D_MODEL)),
        "meta_tokens": nrm(ks[1], (N_META, D_MODEL)),
        "mix_norm": gain(ks[2], (DEPTH, D_MODEL)),
        "dn_w_in": nrm(ks[3], (n_a, D_MODEL, IN_DIM)) * D_MODEL ** -0.5,
        "dn_conv_w": nrm(ks[4], (n_a, CONV_K, QKV_DIM)) * CONV_K ** -0.5,
        "dn_a_log": jnp.log(jax.random.uniform(ks[5], (n_a, NV), jnp.float32, 1.0, 16.0)),
        "dn_dt_bias": 0.1 * nrm(ks[6], (n_a, NV)),
        "dn_out_norm": gain(ks[7], (n_a, DV)),
        "dn_w_out": nrm(ks[8], (n_a, VAL_DIM, D_MODEL)) * VAL_DIM ** -0.5,
        "pool_w": nrm(ks[9], (n_b, N_GROUPS, GROUP_W, GROUP_W)) * GROUP_W ** -0.5,
        "pool_scale": gain(ks[10], (n_b, D_MODEL)),
        "mlp_norm": gain(ks[11], (DEPTH, D_MODEL)),
        "w_up": nrm(ks[12], (DEPTH, D_MODEL, D_FF)) * D_MODEL ** -0.5,
        "w_down": nrm(ks[13], (DEPTH, D_FF, D_MODEL)) * D_FF ** -0.5,
        "final_norm": gain(ks[14], (D_MODEL,)),
    }


def reference(x, meta_tokens, mix_norm, dn_w_in, dn_conv_w, dn_a_log, dn_dt_bias, dn_out_norm,
              dn_w_out, pool_w, pool_scale, mlp_norm, w_up, w_down, final_norm):
    bsz = x.shape[0]
    meta = jnp.broadcast_to(meta_tokens.astype(x.dtype)[None], (bsz, N_META, D_MODEL))
    h = jnp.concatenate([meta, x], axis=1)
    for i in range(DEPTH):
        j = i // N_MIXERS
        hn = rmsnorm(h, mix_norm[i])
        if i % N_MIXERS == 0:
            mix = gated_deltanet(hn, dn_w_in[j], dn_conv_w[j], dn_a_log[j], dn_dt_bias[j],
                                 dn_out_norm[j], dn_w_out[j])
        else:
            mix = multiscale_pool(hn, pool_w[j], pool_scale[j])
        h = h + mix.astype(h.dtype)
        h = h + sq_relu_mlp(rmsnorm(h, mlp_norm[i]), w_up[i], w_down[i]).astype(h.dtype)
    h = rmsnorm(h, final_norm)
    return h[:, N_META:]
```

```python
import numpy as np
import ml_dtypes
from contextlib import ExitStack
import concourse.bass as bass
import concourse.mybir as mybir
from concourse.bass_utils import run_bass_kernel_spmd

F32 = mybir.dt.float32
BF16 = mybir.dt.bfloat16
AF = mybir.ActivationFunctionType
ALU = mybir.AluOpType

EPS = 1e-6
D = 4096
NKC = 32
DFF = 16384
SEQ = 2048
NMETA = 16
LPAD = 2112
TLOC = 1056
TT = 528
HALF = 264
NSLOT = 4


class Sched:
    ENG = ("pe", "dve", "act", "pool", "sp")

    def __init__(self, nc, es):
        self.nc, self.es = nc, es
        self.ops = {e: [] for e in self.ENG}
        self.sems = {}
        for e in self.ENG:
            self.sems["prog_" + e] = es.enter_context(nc.semaphore("prog_" + e))
        self.cnt = {e: 0 for e in self.ENG}
        self.waited = {e: {} for e in self.ENG}
        self.lastw = {}
        self.readers = {}
        self.dcnt = {}
        self.out_tokens = []

    def _deps(self, eng, reads, writes):
        waits = {}

        def need(s, v):
            if eng == "pe" and s == "prog_pe":
                return
            if waits.get(s, 0) < v:
                waits[s] = v
        for k in reads:
            for s, v in self.lastw.get(k, {}).items():
                need(s, v)
        for k in writes:
            for s, v in self.lastw.get(k, {}).items():
                need(s, v)
            for s, v in self.readers.get(k, {}).items():
                need(s, v)
        fin = []
        w = self.waited[eng]
        for s, v in waits.items():
            if w.get(s, 0) < v:
                fin.append((s, v))
                w[s] = v
        return fin

    def _mark(self, tok, reads, writes):
        for k in reads:
            r = self.readers.setdefault(k, {})
            if r.get(tok[0], 0) < tok[1]:
                r[tok[0]] = tok[1]
        for k in writes:
            w = self.lastw.setdefault(k, {})
            if w.get(tok[0], 0) < tok[1]:
                w[tok[0]] = tok[1]
            self.readers[k] = {}

    PSUM_KEYS = ("PA", "PB", "PC", "PT", "PAq", "PAp")

    def _excl(self, reads, writes):
        r2, w2 = [], list(writes)
        for k in reads:
            if isinstance(k, tuple) and k[0] in self.PSUM_KEYS:
                w2.append(k)
            else:
                r2.append(k)
        return r2, w2

    def op(self, eng, fn, reads=(), writes=(), inc=True):
        reads, writes = self._excl(reads, writes)
        fin = self._deps(eng, reads, writes)
        tok = ("prog_" + eng, self.cnt[eng] + 1)
        if inc:
            self.cnt[eng] += 1
        self.ops[eng].append((fin, fn, "prog_" + eng if inc else None, 1))
        self._mark(tok, reads, writes)
        return tok

    def dma(self, q, fn, key, reads=(), writes=(), is_out=False):
        name = "d_" + "".join(ch if ch.isalnum() else "_" for ch in str(key))
        if name not in self.sems:
            self.sems[name] = self.es.enter_context(self.nc.semaphore(name))
            self.dcnt[name] = 0
        fin = self._deps(q, reads, writes)
        self.dcnt[name] += 16
        tok = (name, self.dcnt[name])
        self.ops[q].append((fin, fn, name, 16))
        self._mark(tok, reads, writes)
        if is_out:
            self.out_tokens.append(tok)
        return tok

    def finish(self):
        waits = {}
        for s, v in self.out_tokens:
            waits[s] = max(waits.get(s, 0), v)
        for e in ("pe", "dve", "act", "pool"):
            if self.cnt[e]:
                waits["prog_" + e] = self.cnt[e]
        fin = [(s, v) for s, v in waits.items()]
        self.ops["sp"].append((fin, None, None, 0))

    def emit(self, block):
        def mk(name):
            def body(e):
                for waits, fn, incsem, incv in self.ops[name]:
                    for s, v in waits:
                        e.wait_ge(self.sems[s], v)
                    if fn is None:
                        continue
                    ins = fn(e)
                    if incsem is not None:
                        ins.then_inc(self.sems[incsem], incv)
            return body
        block.tensor(mk("pe"))
        block.vector(mk("dve"))
        block.scalar(mk("act"))
        block.gpsimd(mk("pool"))
        block.sync(mk("sp"))


def build_stageB():
    nc = bass.Bass("TRN2", target_bir_lowering=False)
    es = ExitStack()

    def dram(name, shape, dt, kind="ExternalInput"):
        return nc.dram_tensor(name, shape, dt, kind=kind).ap()

    xT = dram("xT", [128, NKC, TLOC], F32)
    oT = dram("oT", [128, 64, TLOC], BF16)
    w_out = dram("w_out", [8192, D], F32)
    w_up = dram("w_up", [2, D, DFF], F32)
    w_down = dram("w_down", [2, DFF, D], F32)
    pool_w = dram("pool_w", [4, 1024, 1024], F32)
    cvec = dram("cvec", [128, 5, NKC], F32)
    corr = dram("corr", [128, 2, 4, 32], F32)
    outT = dram("outT", [128, NKC, TLOC], F32, kind="ExternalOutput")

    with es:
        S = Sched(nc, es)

        def sb(name, shape, dt):
            return es.enter_context(nc.sbuf_tensor(name, shape, dt))

        h = sb("h", [128, NKC, TT], F32)
        opb = sb("opb", [128, NKC, TT], BF16)
        slots = [sb(f"slot{i}", [128, 8192], BF16) for i in range(NSLOT)]
        r2 = [sb(f"r2_{i}", [128, TT], BF16) for i in range(4)]
        rl = [sb(f"rl{i}", [128, TT], F32) for i in range(2)]
        pscr = [sb(f"pscr{i}", [128, 16 + TT], F32) for i in range(3)]
        rstd = sb("rstd", [128, TT], F32)
        sd = sb("sd", [128, TT], F32)
        carry = sb("carry", [128, NKC, 16], F32)
        cv = sb("cv", [128, 5, NKC], F32)
        corr_sb = sb("corr_sb", [128, 2, 4, 32], F32)
        ones = sb("ones", [128, 128], F32)
        epsb = sb("epsb", [128, 1], F32)
        PA = [es.enter_context(nc.psum_tensor(f"PA{i}", [128, 1024], F32)) for i in range(2)]
        PB = [es.enter_context(nc.psum_tensor(f"PB{i}", [128, 1024], F32)) for i in range(2)]

        def halves(ap2d):
            n = ap2d.shape[-1]
            return ap2d.rearrange("p (b n) -> p b n", b=2)[:, :, 0:HALF]

        S.dma("sp", lambda e: e.dma_start(out=cv[:], in_=cvec[:]), "cv", writes=["cv"])
        S.dma("sp", lambda e: e.dma_start(out=corr_sb[:], in_=corr[:]), "corr", writes=["corr"])
        S.op("pool", lambda e: e.memset(ones[:], 1.0), writes=["ones"])
        S.op("pool", lambda e: e.memset(epsb[:], EPS), writes=["epsb"])
        S.op("pool", lambda e: e.memset(carry[:], 0.0), writes=["carry"])

        slot_ctr = [0]

        def load_w(src_ap, shape3):
            i = slot_ctr[0] % NSLOT
            slot_ctr[0] += 1
            a, b = shape3
            dst = slots[i][:, 0:a * b].rearrange("p (a b) -> p a b", a=a)
            S.dma("pool", lambda e, dst=dst, src=src_ap: e.dma_start(out=dst, in_=src), ("slot", i),
                  writes=[("slot", i)])
            return i, dst

        w_out_v = w_out.rearrange("(kc p) n -> p kc n", p=128)
        w_up_v = [w_up[l].rearrange("(kc p) n -> p kc n", p=128) for l in range(2)]
        w_down_v = [w_down[l].rearrange("(fc p) n -> p fc n", p=128) for l in range(2)]
        pool_w_v = [pool_w[g].rearrange("(kc p) n -> p kc n", p=128) for g in range(4)]

        def mm(out, lhsT, rhs, start, stop, reads, writes, inc):
            S.op("pe", lambda e: e.matmul(out, lhsT=lhsT, rhs=rhs, start=start, stop=stop),
                 reads=reads, writes=writes, inc=inc)

        def rms_rstd():
            for kc in range(NKC):
                sq = rl[kc % 2]
                S.op("act", lambda e, sq=sq, kc=kc: e.activation(out=sq[:], in_=h[:, kc, :], func=AF.Square),
                     reads=[("h", kc)], writes=[("rl", kc % 2)])
                for hf in range(2):
                    mm(PA[0][:, hf * 512:hf * 512 + HALF], ones[:], sq[:, hf * HALF:(hf + 1) * HALF],
                       kc == 0, kc == NKC - 1, [("rl", kc % 2), "ones"], [("PA", 0)], hf == 1)
            S.op("act", lambda e: e.activation(out=halves(sd[:]), in_=halves(PA[0][:]), func=AF.Sqrt,
                                               bias=epsb[:], scale=1.0 / D),
                 reads=[("PA", 0), "epsb"], writes=["sd"])
            S.op("dve", lambda e: e.reciprocal(out=rstd[:], in_=sd[:]), reads=["sd"], writes=["rstd"])

        def norm_to_opb(gidx):
            rms_rstd()
            for kc in range(NKC):
                S.op("dve", lambda e, kc=kc: e.scalar_tensor_tensor(
                    out=opb[:, kc, :], in0=h[:, kc, :], scalar=cv[:, gidx, kc:kc + 1], in1=rstd[:],
                    op0=ALU.mult, op1=ALU.mult),
                    reads=[("h", kc), "cv", "rstd"], writes=[("opb", kc)])

        def mlp(layer, gidx):
            norm_to_opb(gidx)
            NFG = DFF // 256
            st = {}

            def up(fg):
                iu, su = load_w(w_up_v[layer][:, :, fg * 256:(fg + 1) * 256], (NKC, 256))
                idn, sdn = load_w(w_down_v[layer][:, fg * 2:(fg + 1) * 2, :], (2, D))
                st[fg] = (idn, sdn)
                for j in range(2):
                    n = fg * 2 + j
                    pa = PA[n % 2]
                    for hf in range(2):
                        for kc in range(NKC):
                            mm(pa[:, hf * 512:hf * 512 + HALF], su[:, kc, j * 128:(j + 1) * 128],
                               opb[:, kc, hf * HALF:(hf + 1) * HALF], kc == 0, kc == NKC - 1,
                               [("slot", iu), ("opb", kc)], [("PA", n % 2)], hf == 1 and kc == NKC - 1)
                    rr = rl[n % 2]
                    S.op("act", lambda e, rr=rr, pa=pa: e.activation(out=halves(rr[:]), in_=halves(pa[:]), func=AF.Relu),
                         reads=[("PA", n % 2)], writes=[("rl", n % 2)])
                    rb = r2[n % 4]
                    S.op("pool", lambda e, rr=rr, rb=rb: e.tensor_tensor(out=rb[:], in0=rr[:], in1=rr[:], op=ALU.mult),
                         reads=[("rl", n % 2)], writes=[("r2", n % 4)])

            def down(fg):
                idn, sdn = st.pop(fg)
                for db in range(NKC):
                    pb = PB[db % 2]
                    for hf in range(2):
                        for j in range(2):
                            n = fg * 2 + j
                            mm(pb[:, hf * 512:hf * 512 + HALF], sdn[:, j, db * 128:(db + 1) * 128],
                               r2[n % 4][:, hf * HALF:(hf + 1) * HALF], j == 0, j == 1,
                               [("slot", idn), ("r2", n % 4)], [("PB", db % 2)], hf == 1 and j == 1)
                    S.op("dve", lambda e, pb=pb, db=db: e.tensor_tensor(
                        out=halves(h[:, db, :]), in0=halves(pb[:]), in1=halves(h[:, db, :]), op=ALU.add),
                        reads=[("PB", db % 2), ("h", db)], writes=[("h", db)])

            up(0)
            for fg in range(NFG):
                if fg + 1 < NFG:
                    up(fg + 1)
                down(fg)

        for t in range(TLOC // TT):
            t0 = t * TT
            S.dma("sp", lambda e, t0=t0: e.dma_start(out=h[:], in_=xT[:, :, t0:t0 + TT]), "h",
                  writes=[("h", kc) for kc in range(NKC)])
            for kh in range(2):
                S.dma("sp", lambda e, t0=t0, kh=kh: e.dma_start(out=opb[:], in_=oT[:, kh * 32:(kh + 1) * 32, t0:t0 + TT]),
                      "opb", writes=[("opb", kc) for kc in range(NKC)])
                for dg in range(16):
                    iw, sw = load_w(w_out_v[:, kh * 32:(kh + 1) * 32, dg * 256:(dg + 1) * 256], (NKC, 256))
                    for j in range(2):
                        db = dg * 2 + j
                        pb = PB[db % 2]
                        for hf in range(2):
                            for kc in range(NKC):
                                mm(pb[:, hf * 512:hf * 512 + HALF], sw[:, kc, j * 128:(j + 1) * 128],
                                   opb[:, kc, hf * HALF:(hf + 1) * HALF], kc == 0, kc == NKC - 1,
                                   [("slot", iw), ("opb", kc)], [("PB", db % 2)], hf == 1 and kc == NKC - 1)
                        S.op("dve", lambda e, pb=pb, db=db: e.tensor_tensor(
                            out=halves(h[:, db, :]), in0=halves(pb[:]), in1=halves(h[:, db, :]), op=ALU.add),
                            reads=[("PB", db % 2), ("h", db)], writes=[("h", db)])
            mlp(0, 0)
            rms_rstd()
            for kc in range(NKC):
                gi = kc // 8
                win = 2 << gi
                xb = pscr[kc % 3]
                S.op("pool", lambda e, xb=xb, kc=kc: e.tensor_copy(out=xb[:, 0:16], in_=carry[:, kc, :]),
                     reads=["carry"], writes=[("pscr", kc % 3)])
                S.op("dve", lambda e, xb=xb, kc=kc: e.scalar_tensor_tensor(
                    out=xb[:, 16:16 + TT], in0=h[:, kc, :], scalar=cv[:, 1, kc:kc + 1], in1=rstd[:],
                    op0=ALU.mult, op1=ALU.mult),
                    reads=[("h", kc), "cv", "rstd"], writes=[("pscr", kc % 3)])
                S.op("pool", lambda e, xb=xb, kc=kc: e.tensor_copy(out=carry[:, kc, :], in_=xb[:, TT:TT + 16]),
                     reads=[("pscr", kc % 3)], writes=["carry"])
                wa = pscr[(kc + 1) % 3]
                src = xb
                sh = 1
                first = True
                cur = None
                bufs = [pscr[(kc + 1) % 3], pscr[(kc + 2) % 3]]
                bkeys = [("pscr", (kc + 1) % 3), ("pscr", (kc + 2) % 3)]
                srckey = ("pscr", kc % 3)
                bi = 0
                lo_acc = 0
                while sh < win:
                    dst = bufs[bi]
                    dkey = bkeys[bi]
                    lo = lo_acc + sh
                    lo_acc = lo
                    S.op("dve", lambda e, dst=dst, src=src, sh=sh, lo=lo: e.tensor_tensor(
                        out=dst[:, lo:16 + TT], in0=src[:, lo:16 + TT], in1=src[:, lo - sh:16 + TT - sh], op=ALU.add),
                        reads=[srckey], writes=[dkey])
                    src = dst
                    srckey = dkey
                    bi ^= 1
                    sh *= 2
                ws = src
                S.op("pool", lambda e, ws=ws, gi=gi, t=t: e.tensor_tensor(
                    out=ws[:, 16:48], in0=ws[:, 16:48], in1=corr_sb[:, t, gi, :], op=ALU.mult),
                    reads=[srckey, "corr"], writes=[srckey])
                S.op("dve", lambda e, ws=ws, xb=xb, kc=kc, win=win: e.scalar_tensor_tensor(
                    out=opb[:, kc, :], in0=ws[:, 16:16 + TT], scalar=1.0 / win, in1=xb[:, 16:16 + TT],
                    op0=ALU.mult, op1=ALU.subtract),
                    reads=[srckey, ("pscr", kc % 3)], writes=[("opb", kc)])
            for gi in range(4):
                ip, sp_ = load_w(pool_w_v[gi], (8, 1024))
                for j in range(8):
                    db = gi * 8 + j
                    pb = PB[db % 2]
                    for hf in range(2):
                        for kk in range(8):
                            kc = gi * 8 + kk
                            mm(pb[:, hf * 512:hf * 512 + HALF], sp_[:, kk, j * 128:(j + 1) * 128],
                               opb[:, kc, hf * HALF:(hf + 1) * HALF], kk == 0, kk == 7,
                               [("slot", ip), ("opb", kc)], [("PB", db % 2)], hf == 1 and kk == 7)
                    S.op("dve", lambda e, pb=pb, db=db: e.scalar_tensor_tensor(
                        out=halves(h[:, db, :]), in0=halves(pb[:]), scalar=cv[:, 4, db:db + 1], in1=halves(h[:, db, :]),
                        op0=ALU.mult, op1=ALU.add),
                        reads=[("PB", db % 2), ("h", db), "cv"], writes=[("h", db)])
            mlp(1, 2)
            rms_rstd()
            for kc in range(NKC):
                ob = pscr[kc % 3]
                S.op("dve", lambda e, ob=ob, kc=kc: e.scalar_tensor_tensor(
                    out=ob[:, 0:TT], in0=h[:, kc, :], scalar=cv[:, 3, kc:kc + 1], in1=rstd[:],
                    op0=ALU.mult, op1=ALU.mult),
                    reads=[("h", kc), "cv", "rstd"], writes=[("pscr", kc % 3)])
                S.dma("sp", lambda e, ob=ob, kc=kc, t0=t0: e.dma_start(out=outT[:, kc, t0:t0 + TT], in_=ob[:, 0:TT]),
                      ("ost", kc % 3), reads=[("pscr", kc % 3)], is_out=True)
        S.finish()
        block = es.enter_context(nc.Block())
        S.emit(block)
    return nc


NHL = 32
NKL = 16
HG = 8
NCH = LPAD // 64


def build_stageA(do_phase2=True, do_phase1=True, stop_after=99, nhg=NHL // HG, nch=NCH):
    nc = bass.Bass("TRN2", target_bir_lowering=False)
    es = ExitStack()

    def dram(name, shape, dt, kind="ExternalInput"):
        return nc.dram_tensor(name, shape, dt, kind=kind).ap()

    xT = dram("xT", [128, NKC, LPAD], F32)
    wq = dram("wqkvz", [D, 96 * 128], F32)
    wba_d = dram("wba", [D, 64], F32)
    cw_d = dram("convw", [128, 64, 4], F32)
    gv_d = dram("gvec", [128, NKC], F32)
    al_d = dram("alog", [64, NHL], F32)
    dtb_d = dram("dtb", [64, NHL], F32)
    on_d = dram("onorm", [128, 1], F32)
    mk_d = dram("masks", [64, 3, HG, 64], F32)
    id_d = dram("ident", [128, 128], BF16)
    idf_d = dram("identf", [64, HG, 64], F32)
    oT = dram("oT", [128, NHL, LPAD], BF16, kind="ExternalOutput")
    IK = "ExternalOutput"
    QS = dram("QS", [128, NKL, LPAD], BF16, kind=IK)
    KS = dram("KS", [128, NKL, LPAD], BF16, kind=IK)
    VS = dram("VS", [128, NHL, LPAD], BF16, kind=IK)
    ZS = dram("ZS", [128, NHL, LPAD], BF16, kind=IK)
    BA = dram("BA", [LPAD, 64], F32, kind=IK)

    with es:
        S = Sched(nc, es)

        def sb(name, shape, dt):
            return es.enter_context(nc.sbuf_tensor(name, shape, dt))

        def ps(name, shape, dt):
            return es.enter_context(nc.psum_tensor(name, shape, dt))

        def mm(out, lhsT, rhs, start, stop, reads, writes, inc):
            S.op("pe", lambda e: e.matmul(out, lhsT=lhsT, rhs=rhs, start=start, stop=stop),
                 reads=reads, writes=writes, inc=inc)

        def tr(out, in_, idn, reads, writes, inc):
            S.op("pe", lambda e: e.matmul(out, lhsT=in_, rhs=idn, start=True, stop=True), reads=reads, writes=writes, inc=inc)

        def act(out, in_, func, reads, writes, scale=1.0, bias=None):
            if bias is None:
                S.op("act", lambda e: e.activation(out=out, in_=in_, func=func, scale=scale), reads=reads, writes=writes)
            else:
                S.op("act", lambda e: e.activation(out=out, in_=in_, func=func, scale=scale, bias=bias),
                     reads=reads, writes=writes)

        def ts(eng, out, in0, s1, op0, reads, writes, s2=None, op1=None):
            if op1 is None:
                S.op(eng, lambda e: e.tensor_scalar(out=out, in0=in0, scalar1=s1, scalar2=None, op0=op0),
                     reads=reads, writes=writes)
            else:
                S.op(eng, lambda e: e.tensor_scalar(out=out, in0=in0, scalar1=s1, scalar2=s2, op0=op0, op1=op1),
                     reads=reads, writes=writes)

        def tt(eng, out, in0, in1, op, reads, writes):
            S.op(eng, lambda e: e.tensor_tensor(out=out, in0=in0, in1=in1, op=op), reads=reads, writes=writes)

        def stt(out, in0, scalar, in1, op0, op1, reads, writes):
            S.op("dve", lambda e: e.scalar_tensor_tensor(out=out, in0=in0, scalar=scalar, in1=in1, op0=op0, op1=op1),
                 reads=reads, writes=writes)

        def cp(eng, out, in_, reads, writes):
            S.op(eng, lambda e: e.tensor_copy(out=out, in_=in_), reads=reads, writes=writes)

        def halves(ap2d):
            return ap2d.rearrange("p (b n) -> p b n", b=2)[:, :, 0:HALF]

        cw = sb("cw", [128, 64, 4], F32)
        gv = sb("gv", [128, NKC], F32)
        alog = sb("alog_sb", [64, NHL], F32)
        dtb = sb("dtb_sb", [64, NHL], F32)
        nA = sb("nA", [64, NHL], F32)
        onrm = sb("onrm", [128, 1], F32)
        masks = sb("masks_sb", [64, 3, HG, 64], F32)
        ident = sb("ident_sb", [128, 128], BF16)
        identf = sb("identf_sb", [64, HG, 64], F32)
        ones = sb("ones", [128, 128], F32)
        epsb = sb("epsb", [128, 1], F32)
        oneb = sb("oneb", [128, 1], F32)
        wba = sb("wba_sb", [128, NKC, 64], BF16)
        for i, (dst, src) in enumerate([(cw, cw_d), (gv, gv_d), (alog, al_d), (dtb, dtb_d), (onrm, on_d),
                                        (masks, mk_d), (ident, id_d), (identf, idf_d)]):
            S.dma("sp", lambda e, dst=dst, src=src: e.dma_start(out=dst[:], in_=src[:]), ("c", i), writes=[("c", i)])
        CK = [("c", i) for i in range(8)]
        K_cw, K_gv, K_al, K_dtb, K_on, K_mk, K_id, K_idf = CK
        S.dma("pool", lambda e: e.dma_start(out=wba[:], in_=wba_d.rearrange("(kc p) n -> p kc n", p=128)), "wba",
              writes=["wba"])
        S.op("pool", lambda e: e.memset(ones[:], 1.0), writes=["ones"])
        S.op("pool", lambda e: e.memset(epsb[:], EPS), writes=["epsb"])
        S.op("pool", lambda e: e.memset(oneb[:], 1.0), writes=["oneb"])
        act(nA[:], alog[:], AF.Exp, [K_al], ["nA"])
        ts("dve", nA[:], nA[:], -1.0, ALU.mult, ["nA"], ["nA"])
        MUI = masks[:, 0]
        MLS = masks[:, 2]

        PA = [ps(f"PA{i}", [128, 1024], F32) for i in range(2)]
        PC = [ps(f"PC{i}", [128, 512], F32) for i in range(2)]
        PT = [ps(f"PT{i}", [128, 512], F32) for i in range(2)]

        hn = sb("hn", [128, NKC, TT], BF16)
        slots = [sb(f"slot{i}", [128, 8192], BF16) for i in range(3)]
        xbuf = [sb(f"xbuf{i}", [128, TT], F32) for i in range(2)]
        sqb = sb("sqb", [128, TT], F32)
        rstd = sb("rstd", [128, TT], F32)
        sd = sb("sd", [128, TT], F32)
        pre = [sb(f"pre{i}", [128, TT + 3], F32) for i in range(2)]
        acc = sb("acc", [128, TT], F32)
        sil = sb("sil", [128, TT], F32)
        outb = [sb(f"outb{i}", [128, TT], BF16) for i in range(2)]
        carryc = sb("carryc", [128, 64, 3], F32)
        babuf = [sb(f"babuf{i}", [66, 64], F32) for i in range(2)]
        S.op("pool", lambda e: e.memset(carryc[:], 0.0), writes=["carryc"])
        KPA = [[("PA", 0)], [("PAq", 1), ("PAp", 1)]]
        wq_v = wq.rearrange("(kc p) n -> p kc n", p=128)
        slot_ctr = [0]
        ob_ctr = [0]

        def store_out(dst_dram, blk_idx, t0, key):
            i = ob_ctr[0] % 2
            ob_ctr[0] += 1
            return i

        for t in range(LPAD // TT if do_phase1 else 0):
            t0 = t * TT
            for kc in range(NKC):
                xb = xbuf[kc % 2]
                S.dma("sp", lambda e, xb=xb, kc=kc, t0=t0: e.dma_start(out=xb[:], in_=xT[:, kc, t0:t0 + TT]),
                      ("xbuf", kc % 2), writes=[("xbuf", kc % 2)])
                act(sqb[:], xb[:], AF.Square, [("xbuf", kc % 2)], ["sqb"])
                for hf in range(2):
                    mm(PA[0][:, hf * 512:hf * 512 + HALF], ones[:], sqb[:, hf * HALF:(hf + 1) * HALF],
                       kc == 0, kc == NKC - 1, ["sqb", "ones"], [("PA", 0)], hf == 1)
            act(halves(sd[:]), halves(PA[0][:]), AF.Sqrt, [("PA", 0), "epsb"], ["sd"], scale=1.0 / D, bias=epsb[:])
            S.op("dve", lambda e: e.reciprocal(out=rstd[:], in_=sd[:]), reads=["sd"], writes=["rstd"])
            for kc in range(NKC):
                xb = xbuf[kc % 2]
                S.dma("sp", lambda e, xb=xb, kc=kc, t0=t0: e.dma_start(out=xb[:], in_=xT[:, kc, t0:t0 + TT]),
                      ("xbuf", kc % 2), writes=[("xbuf", kc % 2)])
                stt(hn[:, kc, :], xb[:], gv[:, kc:kc + 1], rstd[:], ALU.mult, ALU.mult,
                    [("xbuf", kc % 2), K_gv, "rstd"], [("hn", kc)])
            for sbk in range(TT // 66):
                pc = PC[sbk % 2]
                for kc in range(NKC):
                    mm(pc[0:66, 0:64], hn[:, kc, sbk * 66:(sbk + 1) * 66], wba[:, kc, :], kc == 0, kc == NKC - 1,
                       [("hn", kc), "wba"], [("PC", sbk % 2)], kc == NKC - 1)
                bb = babuf[sbk % 2]
                act(bb[:], pc[0:66, 0:64], AF.Copy, [("PC", sbk % 2)], [("babuf", sbk % 2)])
                S.dma("sp", lambda e, bb=bb, r0=t0 + sbk * 66: e.dma_start(out=BA[r0:r0 + 66, :], in_=bb[:]),
                      ("babuf", sbk % 2), reads=[("babuf", sbk % 2)], writes=["BA"], is_out=True)
            for pair in range(48):
                si = slot_ctr[0] % 3
                slot_ctr[0] += 1
                sw = slots[si][:, :].rearrange("p (a b) -> p a b", a=NKC)
                S.dma("pool", lambda e, sw=sw, pair=pair: e.dma_start(out=sw, in_=wq_v[:, :, pair * 256:(pair + 1) * 256]),
                      ("slot", si), writes=[("slot", si)])
                for j in range(2):
                    blk = pair * 2 + j
                    pa = PA[blk % 2]
                    for hf in range(2):
                        for kc in range(NKC):
                            mm(pa[:, hf * 512:hf * 512 + HALF], sw[:, kc, j * 128:(j + 1) * 128],
                               hn[:, kc, hf * HALF:(hf + 1) * HALF], kc == 0, kc == NKC - 1,
                               [("slot", si), ("hn", kc)], KPA[blk % 2], hf == 1 and kc == NKC - 1)
                    oi = ob_ctr[0] % 2
                    ob_ctr[0] += 1
                    ob = outb[oi]
                    if blk < 64:
                        pr = pre[blk % 2]
                        kpr = ("pre", blk % 2)
                        cp("pool", pr[:, 0:3], carryc[:, blk, :], ["carryc"], [kpr])
                        act(halves(pr[:, 3:3 + TT]), halves(pa[:]), AF.Copy, KPA[blk % 2], [kpr])
                        cp("pool", carryc[:, blk, :], pr[:, TT:TT + 3], [kpr], ["carryc"])
                        ts("dve", acc[:], pr[:, 0:TT], cw[:, blk, 0:1], ALU.mult, [kpr, K_cw], ["acc"])
                        for tap in range(1, 4):
                            stt(acc[:], pr[:, tap:tap + TT], cw[:, blk, tap:tap + 1], acc[:], ALU.mult, ALU.add,
                                [kpr, K_cw, "acc"], ["acc"])
                        if blk < 32:
                            act(sil[:], acc[:], AF.Silu, ["acc"], ["sil"])
                            act(sqb[:], sil[:], AF.Square, ["sil"], ["sqb"])
                            pl = PA[blk % 2]
                            for hf in range(2):
                                mm(pl[:, hf * 512:hf * 512 + HALF], ones[:], sqb[:, hf * HALF:(hf + 1) * HALF], True, True,
                                   ["sqb", "ones"], KPA[blk % 2], hf == 1)
                            act(halves(sd[:]), halves(pl[:]), AF.Sqrt, KPA[blk % 2] + ["epsb"], ["sd"], bias=epsb[:])
                            S.op("dve", lambda e: e.reciprocal(out=rstd[:], in_=sd[:]), reads=["sd"], writes=["rstd"])
                            stt(ob[:], sil[:], (128.0 ** -0.5) if blk < 16 else 1.0, rstd[:], ALU.mult, ALU.mult,
                                ["sil", "rstd"], [("outb", oi)])
                            dst = (QS if blk < 16 else KS)[:, blk % 16, t0:t0 + TT]
                            dkey = ("QS" if blk < 16 else "KS")
                        else:
                            act(ob[:], acc[:], AF.Silu, ["acc"], [("outb", oi)])
                            dst = VS[:, blk - 32, t0:t0 + TT]
                            dkey = "VS"
                    else:
                        act(halves(ob[:]), halves(pa[:]), AF.Silu, KPA[blk % 2], [("outb", oi)])
                        dst = ZS[:, blk - 64, t0:t0 + TT]
                        dkey = "ZS"
                    S.dma("sp", lambda e, dst=dst, ob=ob: e.dma_start(out=dst, in_=ob[:]), ("outb", oi),
                          reads=[("outb", oi)], writes=[dkey], is_out=True)

        Sst = sb("Sst", [128, HG, 128], F32)
        Sbf = sb("Sbf", [128, HG, 128], BF16)
        qc = [sb(f"qc{i}", [128, 4, 64], BF16) for i in range(2)]
        kcb = [sb(f"kcb{i}", [128, 4, 64], BF16) for i in range(2)]
        vc = [sb(f"vc{i}", [128, HG, 64], BF16) for i in range(2)]
        zc = [sb(f"zc{i}", [128, HG, 64], BF16) for i in range(2)]
        bac = [sb(f"bac{i}", [64, 64], F32) for i in range(2)]
        sm = {n: sb("sm_" + n, [64, HG], F32) for n in ("e1", "beta", "xa", "ax", "e2", "ln", "g", "gc", "tmp", "kds", "egc", "bg", "r1", "r2")}
        egl = sb("egl", [128, HG], F32)
        LGb = [sb(f"LGb{i}", [64, HG, 64], BF16) for i in range(3)]
        gpc = [sb(f"gpc{i}", [64, HG], BF16) for i in range(3)]
        MUIb = sb("MUIb", [64, 64], BF16)
        onesb = sb("onesb", [128, 128], BF16)
        S.op("pool", lambda e: e.memset(onesb[:], 1.0), writes=["onesb"])
        cp("pool", MUIb[:], MUI[:, 0, :], [K_mk], ["MUIb"])
        DL = sb("DL", [64, HG, 64], F32)
        EU = sb("EU", [64, HG, 64], F32)
        EL = sb("EL", [64, HG, 64], F32)
        Abf = sb("Abf", [64, HG, 64], BF16)
        Pb = [sb(f"Pb{i}", [64, HG, 64], BF16) for i in range(2)]
        Qb = [sb(f"Qb{i}", [64, HG, 64], BF16) for i in range(2)]
        Rf = sb("Rf", [64, HG, 64], F32)
        Rb = sb("Rb", [64, HG, 64], BF16)
        Kbg = sb("Kbg", [64, HG, 128], BF16)
        Kd = sb("Kd", [64, HG, 128], BF16)
        Vb = sb("Vb", [64, HG, 128], BF16)
        WTb = sb("WTb", [128, HG, 64], BF16)
        Usb = sb("Usb", [64, HG, 128], F32)
        Vn = sb("Vn", [64, HG, 128], BF16)
        QKm = sb("QKm", [64, HG, 64], BF16)
        EG = sb("EG", [128, HG, 64], F32)
        QdT = sb("QdT", [128, HG, 64], BF16)
        Oc = sb("Oc", [128, HG, 64], F32)
        Osq = sb("Osq", [128, HG, 64], F32)
        Ord = sb("Ord", [128, HG, 64], F32)
        Ob = [sb(f"Ob{i}", [128, HG, 64], BF16) for i in range(2)]

        def v3(ap2d, a):
            return ap2d.rearrange("p (a b) -> p a b", a=a)

        for hg in range(nhg if do_phase2 else 0):
            hb = hg * HG
            S.op("pool", lambda e: e.memset(Sst[:], 0.0), reads=[], writes=["Sst"])
            S.op("pool", lambda e: e.memset(Sbf[:], 0.0), reads=[], writes=["Sbf"])
            def load_chunk(c, hg=hg, hb=hb):
                c0 = c * 64
                bi = c % 2
                S.dma("sp", lambda e: e.dma_start(out=qc[bi][:], in_=QS[:, hg * 4:hg * 4 + 4, c0:c0 + 64]),
                      ("qc", bi), reads=["QS"], writes=[("qc", bi)])
                S.dma("sp", lambda e: e.dma_start(out=kcb[bi][:], in_=KS[:, hg * 4:hg * 4 + 4, c0:c0 + 64]),
                      ("kcb", bi), reads=["KS"], writes=[("kcb", bi)])
                S.dma("sp", lambda e: e.dma_start(out=vc[bi][:], in_=VS[:, hb:hb + HG, c0:c0 + 64]),
                      ("vc", bi), reads=["VS"], writes=[("vc", bi)])
                S.dma("sp", lambda e: e.dma_start(out=zc[bi][:], in_=ZS[:, hb:hb + HG, c0:c0 + 64]),
                      ("zc", bi), reads=["ZS"], writes=[("zc", bi)])
                S.dma("sp", lambda e: e.dma_start(out=bac[bi][:], in_=BA[c0:c0 + 64, :]),
                      ("bac", bi), reads=["BA"], writes=[("bac", bi)])
            load_chunk(0)
            for c in range(nch):
                c0 = c * 64
                bi = c % 2
                if c + 1 < nch:
                    load_chunk(c + 1)
                q_, k_, v_, z_, ba_ = qc[bi], kcb[bi], vc[bi], zc[bi], bac[bi]
                Kq, Kk, Kv, Kz, Kba = ("qc", bi), ("kcb", bi), ("vc", bi), ("zc", bi), ("bac", bi)
                act(sm["e1"][:], ba_[:, hb:hb + HG], AF.Exp, [Kba], ["e1"], scale=-1.0)
                ts("dve", sm["e1"][:], sm["e1"][:], 1.0, ALU.add, ["e1"], ["e1"])
                S.op("dve", lambda e: e.reciprocal(out=sm["beta"][:], in_=sm["e1"][:]), reads=["e1"], writes=["beta"])
                tt("dve", sm["xa"][:], ba_[:, 32 + hb:32 + hb + HG], dtb[:, hb:hb + HG], ALU.add, [Kba, K_dtb], ["xa"])
                stt(sm["ax"][:], sm["xa"][:], -1.0, sm["xa"][:], ALU.mult, ALU.max, ["xa"], ["ax"])
                act(sm["e2"][:], sm["ax"][:], AF.Exp, ["ax"], ["e2"], scale=-1.0)
                act(sm["ln"][:], sm["e2"][:], AF.Ln, ["e2", "oneb"], ["ln"], bias=oneb[0:64, :])
                stt(sm["g"][:], sm["xa"][:], 0.0, sm["ln"][:], ALU.max, ALU.add, ["xa", "ln"], ["g"])
                tt("dve", sm["g"][:], sm["g"][:], nA[:, hb:hb + HG], ALU.mult, ["g", "nA"], ["g"])
                mm(PC[0][0:64, 0:HG], MUI[:, 0, :], sm["g"][:], True, True, [K_mk, "g"], [("PC", 0)], False)
                mm(PC[0][:, 64:64 + HG], ones[0:64, :], sm["g"][:], True, True, ["ones", "g"], [("PC", 0)], True)
                cp("dve", sm["gc"][:], PC[0][0:64, 0:HG], [("PC", 0)], ["gc"])
                act(egl[:], PC[0][:, 64:64 + HG], AF.Exp, [("PC", 0)], ["egl"])
                tt("dve", sm["tmp"][:], PC[0][0:64, 64:64 + HG], sm["gc"][:], ALU.subtract, [("PC", 0), "gc"], ["tmp"])
                act(sm["kds"][:], sm["tmp"][:], AF.Exp, ["tmp"], ["kds"])
                act(sm["egc"][:], sm["gc"][:], AF.Exp, ["gc"], ["egc"])
                tt("dve", sm["bg"][:], sm["egc"][:], sm["beta"][:], ALU.mult, ["egc", "beta"], ["bg"])
                if stop_after < 1.1:
                    continue
                pkq = v3(PC[1][0:64, :], 8)
                for kh in range(4):
                    mm(pkq[:, kh, :], k_[:, kh, :], k_[:, kh, :], True, True, [Kk], [("PC", 1)], False)
                    mm(pkq[:, 4 + kh, :], k_[:, kh, :], q_[:, kh, :], True, True, [Kk, Kq], [("PC", 1)], kh == 3)
                pkt = v3(PT[0][0:64, :], 4)
                pvt = v3(PA[1][0:64, :], 8)
                for kh in range(4):
                    tr(pkt[:, kh, :], k_[:, kh, :], ident[:], [Kk, K_id], [("PT", 0)], kh == 3)
                for j in range(HG):
                    tr(pvt[:, j, :], v_[:, j, :], ident[:], [Kv, K_id], [("PAq", 1), ("PAp", 1)], j == HG - 1)
                if stop_after < 1.25:
                    continue
                cp("dve", gpc[0][:], sm["g"][:], ["g"], ["gp0"])
                tt("dve", sm["r1"][:], sm["g"][:], gpc[0][:], ALU.subtract, ["g", "gp0"], ["r1"])
                cp("dve", gpc[1][:], sm["r1"][:], ["r1"], ["gp1"])
                tt("dve", sm["r2"][:], sm["r1"][:], gpc[1][:], ALU.subtract, ["r1", "gp1"], ["r2"])
                cp("dve", gpc[2][:], sm["r2"][:], ["r2"], ["gp2"])
                for k3 in range(3):
                    for j in range(HG):
                        ts("dve", LGb[k3][:, j, :], MUIb[:], gpc[k3][:, j:j + 1], ALU.mult, ["MUIb", "gp%d" % k3], [("LGb", k3)])
                pgr = v3(PA[0][:, 0:512], 8)
                for j in range(HG):
                    for k3 in range(3):
                        mm(pgr[:, j, :], onesb[0:64, :], LGb[k3][:, j, :], k3 == 0, k3 == 2, ["onesb", ("LGb", k3)],
                           [("PA", 0)], j == HG - 1 and k3 == 2)
                if stop_after < 1.5:
                    continue
                for j in range(HG):
                    ts("dve", DL[:, j, :], pgr[0:64, j, :], sm["gc"][:, j:j + 1], ALU.subtract, [("PA", 0), "gc"], ["DL"])
                act(EG[:], pgr[:], AF.Exp, [("PA", 0)], ["EG"])
                if stop_after < 1.75:
                    continue
                ts("dve", EU[:], DL[:], 0.0, ALU.min, ["DL"], ["EU"])
                act(EU[:], EU[:], AF.Exp, ["EU"], ["EU"])
                tt("pool", EU[:], EU[:], MUI, ALU.mult, ["EU", K_mk], ["EU"])
                ts("dve", EL[:], DL[:], -1.0, ALU.mult, ["DL"], ["EL"], s2=0.0, op1=ALU.min)
                act(EL[:], EL[:], AF.Exp, ["EL"], ["EL"])
                tt("pool", EL[:], EL[:], MLS, ALU.mult, ["EL", K_mk], ["EL"])
                if stop_after < 3:
                    continue
                for j in range(HG):
                    stt(Abf[:, j, :], pkq[:, j // 2, :], sm["beta"][:, j:j + 1], EL[:, j, :], ALU.mult, ALU.mult,
                        [("PC", 1), "beta", "EL"], ["Abf"])
                for j in range(HG):
                    tt("dve", QKm[:, j, :], pkq[:, 4 + j // 2, :], EU[:, j, :], ALU.mult, [("PC", 1), "EU"], ["QKm"])
                for j in range(HG):
                    ts("dve", Kbg[:, j, :], pkt[:, j // 2, :], sm["bg"][:, j:j + 1], ALU.mult, [("PT", 0), "bg"], ["Kbg"])
                    ts("dve", Kd[:, j, :], pkt[:, j // 2, :], sm["kds"][:, j:j + 1], ALU.mult, [("PT", 0), "kds"], ["Kd"])
                    ts("dve", Vb[:, j, :], pvt[:, j, :], sm["beta"][:, j:j + 1], ALU.mult, [("PAq", 1), ("PAp", 1), "beta"], ["Vb"])
                for j in range(HG):
                    tt("pool", QdT[:, j, :], q_[:, j // 2, :], EG[:, j, :], ALU.mult, [Kq, "EG"], ["QdT"])
                pbt = v3(PT[1][0:64, :], 8)
                for j in range(HG):
                    tr(pbt[:, j, :], Abf[:, j, :], ident[0:64, 0:64], ["Abf", K_id], [("PT", 1)], j == HG - 1)
                cp("dve", Pb[0][:], pbt[:], [("PT", 1)], [("Pb", 0)])
                cp("pool", Qb[0][:], Abf[:], ["Abf"], [("Qb", 0)])
                tt("dve", Rf[:], identf[:], pbt[:], ALU.subtract, [K_idf, ("PT", 1)], ["Rf"])
                cp("pool", Rb[:], Rf[:], ["Rf"], ["Rb"])
                for lv in range(1, 6):
                    pi, po = (lv - 1) % 2, lv % 2
                    pq = v3(PA[1][0:64, 0:512], 8)
                    pp = v3(PA[1][0:64, 512:1024], 8)
                    for j in range(HG):
                        mm(pq[:, j, :], Pb[pi][:, j, :], Qb[pi][:, j, :], True, True, [("Pb", pi), ("Qb", pi)], [("PAq", 1)],
                           j == HG - 1)
                    if lv < 5:
                        for j in range(HG):
                            mm(pp[:, j, :], Qb[pi][:, j, :], Pb[pi][:, j, :], True, True, [("Pb", pi), ("Qb", pi)],
                               [("PAp", 1)], j == HG - 1)
                    cp("dve", Qb[po][:], pq[:], [("PAq", 1)], [("Qb", po)])
                    if lv < 5:
                        act(Pb[po][:], pp[:], AF.Copy, [("PAp", 1)], [("Pb", po)])
                    pr_ = v3(PC[0][0:64, :], 8)
                    for j in range(HG):
                        mm(pr_[:, j, :], Qb[po][:, j, :], Rb[:, j, :], True, True, [("Qb", po), "Rb"], [("PC", 0)], j == HG - 1)
                    tt("dve", Rf[:], Rf[:], pr_[:], ALU.add, ["Rf", ("PC", 0)], ["Rf"])
                    cp("pool", Rb[:], Rf[:], ["Rf"], ["Rb"])
                if stop_after < 4:
                    continue
                pwt = v3(PA[0][:, 0:512], 8)
                for j in range(HG):
                    mm(pwt[:, j, :], Kbg[:, j, :], Rb[:, j, :], True, True, ["Kbg", "Rb"], [("PA", 0)], j == HG - 1)
                act(WTb[:], pwt[:], AF.Copy, [("PA", 0)], ["WTb"])
                pu = v3(PA[1][0:64, :], 8)
                for j in range(HG):
                    mm(pu[:, j, :], Rb[:, j, :], Vb[:, j, :], True, True, ["Rb", "Vb"], [("PAq", 1), ("PAp", 1)], j == HG - 1)
                act(Usb[:], pu[:], AF.Copy, [("PAq", 1), ("PAp", 1)], ["Usb"])
                pws = v3(PA[1][0:64, :], 8)
                for j in range(HG):
                    mm(pws[:, j, :], WTb[:, j, :], Sbf[:, j, :], True, True, ["WTb", "Sbf"], [("PAq", 1), ("PAp", 1)], j == HG - 1)
                tt("dve", Vn[:], Usb[:], pws[:], ALU.subtract, ["Usb", ("PAq", 1), ("PAp", 1)], ["Vn"])
                po_ = v3(PC[1][:, :], 8)
                for j in range(HG):
                    mm(po_[:, j, :], Vn[:, j, :], QKm[:, j, :], True, False, ["Vn", "QKm"], [("PC", 1)], False)
                    mm(po_[:, j, :], Sbf[:, j, :], QdT[:, j, :], False, True, ["Sbf", "QdT"], [("PC", 1)], j == HG - 1)
                pss = v3(PA[0][:, :], 8)
                for j in range(HG):
                    mm(pss[:, j, :], Kd[:, j, :], Vn[:, j, :], True, True, ["Kd", "Vn"], [("PA", 0)], j == HG - 1)
                for j in range(HG):
                    stt(Sst[:, j, :], Sst[:, j, :], egl[:, j:j + 1], pss[:, j, :], ALU.mult, ALU.add,
                        ["Sst", "egl", ("PA", 0)], ["Sst"])
                cp("pool", Sbf[:], Sst[:], ["Sst"], ["Sbf"])
                act(Oc[:], po_[:], AF.Copy, [("PC", 1)], ["Oc"])
                act(Osq[:], Oc[:], AF.Square, ["Oc"], ["Osq"])
                pn = PC[0][:, :]
                mm(pn, ones[:], Osq[:].rearrange("p a b -> p (a b)"), True, True, ["ones", "Osq"], [("PC", 0)], True)
                act(Osq[:].rearrange("p a b -> p (a b)"), pn, AF.Sqrt, [("PC", 0), "epsb"], ["Osq"], scale=1.0 / 128, bias=epsb[:])
                S.op("dve", lambda e: e.reciprocal(out=Ord[:], in_=Osq[:]), reads=["Osq"], writes=["Ord"])
                stt(Oc[:].rearrange("p a b -> p (a b)"), Oc[:].rearrange("p a b -> p (a b)"), onrm[:, 0:1],
                    Ord[:].rearrange("p a b -> p (a b)"), ALU.mult, ALU.mult, ["Oc", K_on, "Ord"], ["Oc"])
                tt("dve", Ob[bi][:], Oc[:], z_[:], ALU.mult, ["Oc", Kz], [("Ob", bi)])
                S.dma("sp", lambda e, bi=bi, hb=hb, c0=c0: e.dma_start(out=oT[:, hb:hb + HG, c0:c0 + 64], in_=Ob[bi][:]),
                      ("Ob", bi), reads=[("Ob", bi)], is_out=True)
        S.finish()
        block = es.enter_context(nc.Block())
        S.emit(block)
    return nc


def _fm(a2d):
    t, f = a2d.shape
    return np.ascontiguousarray(a2d.T.reshape(f // 128, 128, t).transpose(1, 0, 2))


def _vec(v):
    return np.ascontiguousarray(v.reshape(-1, 128).T)


def _seq_pad(inp, b):
    return np.concatenate([np.zeros((LPAD - SEQ - NMETA, D), np.float32),
                           np.asarray(inp["meta_tokens"], np.float32),
                           np.asarray(inp["x"][b], np.float32)], axis=0)


def _corr_table(th):
    c = np.ones((128, 2, 4, 32), np.float32)
    if th == 0:
        for gi, win in enumerate((2, 4, 8, 16)):
            for i in range(16, 32):
                pos = i - 15
                c[:, 0, gi, i] = float(win) / float(min(pos, win))
    return c


def stageB_inputs(inp, og_pad_bf16):
    cvec = np.stack([_vec(inp["mlp_norm"][0]), _vec(inp["mix_norm"][1]), _vec(inp["mlp_norm"][1]),
                     _vec(inp["final_norm"]), _vec(inp["pool_scale"][0])], axis=1).astype(np.float32)
    shared = {"w_out": np.asarray(inp["dn_w_out"][0]), "w_up": np.asarray(inp["w_up"]),
              "w_down": np.asarray(inp["w_down"]), "pool_w": np.asarray(inp["pool_w"][0]),
              "cvec": np.ascontiguousarray(cvec)}
    maps = []
    for c in range(8):
        b, th = c // 2, c % 2
        s0 = 32 if th == 0 else LPAD - TLOC
        sp = _seq_pad(inp, b)[s0:s0 + TLOC]
        m = dict(shared)
        m["xT"] = _fm(sp)
        m["oT"] = _fm(og_pad_bf16[b, s0:s0 + TLOC])
        m["corr"] = _corr_table(th)
        maps.append(m)
    return maps


def stageB_gather(outs):
    res = np.zeros((4, SEQ, D), np.float32)
    for c in range(8):
        b, th = c // 2, c % 2
        o = outs[c]
        tok = o.transpose(2, 1, 0).reshape(TLOC, D)
        res[b, th * 1024:(th + 1) * 1024] = tok[32:]
    return res


def _masks():
    p = np.arange(64)[:, None]
    n = np.arange(64)[None, :]
    m = np.stack([(p <= n), (p < n), (n < p)], 0).astype(np.float32)
    m = np.broadcast_to(m[:, :, None, :], (3, 64, HG, 64)).transpose(1, 0, 2, 3)
    return np.ascontiguousarray(m)


def stageA_inputs(inp):
    w_in = np.asarray(inp["dn_w_in"][0])
    conv_w = np.asarray(inp["dn_conv_w"][0])
    maps = []
    identf = np.ascontiguousarray(np.broadcast_to(np.eye(64, dtype=np.float32)[:, None, :], (64, HG, 64)))
    shared = {"gvec": _vec(np.asarray(inp["mix_norm"][0], np.float32)),
              "onorm": np.ascontiguousarray(np.asarray(inp["dn_out_norm"][0], np.float32)[:, None]),
              "masks": _masks(), "ident": np.eye(128, dtype=np.float32).astype(ml_dtypes.bfloat16),
              "identf": identf}
    per_hh = {}
    for hh in range(2):
        q0, k0, v0, z0 = 16 * hh * 128, 4096 + 16 * hh * 128, 8192 + 32 * hh * 128, 16384 + 32 * hh * 128
        wqkvz = np.ascontiguousarray(np.concatenate(
            [w_in[:, q0:q0 + 2048], w_in[:, k0:k0 + 2048], w_in[:, v0:v0 + 4096], w_in[:, z0:z0 + 4096]], axis=1))
        b0, a0 = 24576 + 32 * hh, 24576 + 64 + 32 * hh
        wba = np.ascontiguousarray(np.concatenate([w_in[:, b0:b0 + 32], w_in[:, a0:a0 + 32]], axis=1))
        ch = np.concatenate([np.arange(q0, q0 + 2048), np.arange(k0, k0 + 2048), np.arange(v0, v0 + 4096)])
        cw = conv_w[:, ch]
        convw = np.ascontiguousarray(cw.reshape(4, 64, 128).transpose(2, 1, 0))
        alog = np.ascontiguousarray(np.broadcast_to(np.asarray(inp["dn_a_log"][0])[None, 32 * hh:32 * hh + 32], (64, 32)))
        dtb = np.ascontiguousarray(np.broadcast_to(np.asarray(inp["dn_dt_bias"][0])[None, 32 * hh:32 * hh + 32], (64, 32)))
        per_hh[hh] = {"wqkvz": wqkvz, "wba": wba, "convw": convw, "alog": alog.astype(np.float32),
                      "dtb": dtb.astype(np.float32)}
    xts = {}
    for c in range(8):
        b, hh = c // 2, c % 2
        if b not in xts:
            xts[b] = _fm(_seq_pad(inp, b))
        m = dict(shared)
        m.update(per_hh[hh])
        m["xT"] = xts[b]
        maps.append(m)
    return maps


def stageA_gather(outs):
    og = np.zeros((4, LPAD, 8192), ml_dtypes.bfloat16)
    for c in range(8):
        b, hh = c // 2, c % 2
        o = np.asarray(outs[c])
        og[b, :, hh * 4096:(hh + 1) * 4096] = o.transpose(2, 1, 0).reshape(LPAD, 4096)
    return og


_NC_CACHE = {}


def kernel(**inputs):
    inp = {k: np.asarray(v) for k, v in inputs.items()}
    if "A" not in _NC_CACHE:
        _NC_CACHE["A"] = build_stageA()
        _NC_CACHE["B"] = build_stageB()
    resA = run_bass_kernel_spmd(_NC_CACHE["A"], stageA_inputs(inp), core_ids=list(range(8)))
    og = stageA_gather([r["oT"] for r in resA.results])
    resB = run_bass_kernel_spmd(_NC_CACHE["B"], stageB_inputs(inp, og), core_ids=list(range(8)))
    return stageB_gather([r["outT"] for r in resB.results])
```

```python
import numpy as np
import ml_dtypes
from contextlib import ExitStack
import concourse.bass as bass
import concourse.mybir as mybir
from concourse.bass_utils import run_bass_kernel_spmd

F32 = mybir.dt.float32
BF16 = mybir.dt.bfloat16
AF = mybir.ActivationFunctionType
ALU = mybir.AluOpType

EPS = 1e-6
D = 4096
NKC = 32
DFF = 16384
SEQ = 2048
NMETA = 16
LPAD = 2112
TLOC = 1056
TT = 528
HALF = 264
NSLOT = 4


class Sched:
    ENG = ("pe", "dve", "act", "pool", "sp")

    def __init__(self, nc, es):
        self.nc, self.es = nc, es
        self.ops = {e: [] for e in self.ENG}
        self.sems = {}
        for e in self.ENG:
            self.sems["prog_" + e] = es.enter_context(nc.semaphore("prog_" + e))
        self.cnt = {e: 0 for e in self.ENG}
        self.waited = {e: {} for e in self.ENG}
        self.lastw = {}
        self.readers = {}
        self.dcnt = {}
        self.out_tokens = []

    def _deps(self, eng, reads, writes):
        waits = {}

        def need(s, v):
            if eng == "pe" and s == "prog_pe":
                return
            if waits.get(s, 0) < v:
                waits[s] = v
        for k in reads:
            for s, v in self.lastw.get(k, {}).items():
                need(s, v)
        for k in writes:
            for s, v in self.lastw.get(k, {}).items():
                need(s, v)
            for s, v in self.readers.get(k, {}).items():
                need(s, v)
        fin = []
        w = self.waited[eng]
        for s, v in waits.items():
            if w.get(s, 0) < v:
                fin.append((s, v))
                w[s] = v
        return fin

    def _mark(self, tok, reads, writes):
        for k in reads:
            r = self.readers.setdefault(k, {})
            if r.get(tok[0], 0) < tok[1]:
                r[tok[0]] = tok[1]
        for k in writes:
            w = self.lastw.setdefault(k, {})
            if w.get(tok[0], 0) < tok[1]:
                w[tok[0]] = tok[1]
            self.readers[k] = {}

    PSUM_KEYS = ("PA", "PB", "PC", "PT", "PAq", "PAp")

    def _excl(self, reads, writes):
        r2, w2 = [], list(writes)
        for k in reads:
            if isinstance(k, tuple) and k[0] in self.PSUM_KEYS:
                w2.append(k)
            else:
                r2.append(k)
        return r2, w2

    def op(self, eng, fn, reads=(), writes=(), inc=True):
        reads, writes = self._excl(reads, writes)
        fin = self._deps(eng, reads, writes)
        tok = ("prog_" + eng, self.cnt[eng] + 1)
        if inc:
            self.cnt[eng] += 1
        self.ops[eng].append((fin, fn, "prog_" + eng if inc else None, 1))
        self._mark(tok, reads, writes)
        return tok

    def dma(self, q, fn, key, reads=(), writes=(), is_out=False):
        name = "d_" + "".join(ch if ch.isalnum() else "_" for ch in str(key))
        if name not in self.sems:
            self.sems[name] = self.es.enter_context(self.nc.semaphore(name))
            self.dcnt[name] = 0
        fin = self._deps(q, reads, writes)
        self.dcnt[name] += 16
        tok = (name, self.dcnt[name])
        self.ops[q].append((fin, fn, name, 16))
        self._mark(tok, reads, writes)
        if is_out:
            self.out_tokens.append(tok)
        return tok

    def finish(self):
        waits = {}
        for s, v in self.out_tokens:
            waits[s] = max(waits.get(s, 0), v)
        for e in ("pe", "dve", "act", "pool"):
            if self.cnt[e]:
                waits["prog_" + e] = self.cnt[e]
        fin = [(s, v) for s, v in waits.items()]
        self.ops["sp"].append((fin, None, None, 0))

    def emit(self, block):
        def mk(name):
            def body(e):
                for waits, fn, incsem, incv in self.ops[name]:
                    for s, v in waits:
                        e.wait_ge(self.sems[s], v)
                    if fn is None:
                        continue
                    ins = fn(e)
                    if incsem is not None:
                        ins.then_inc(self.sems[incsem], incv)
            return body
        block.tensor(mk("pe"))
        block.vector(mk("dve"))
        block.scalar(mk("act"))
        block.gpsimd(mk("pool"))
        block.sync(mk("sp"))


def build_stageB():
    nc = bass.Bass("TRN2", target_bir_lowering=False)
    es = ExitStack()

    def dram(name, shape, dt, kind="ExternalInput"):
        return nc.dram_tensor(name, shape, dt, kind=kind).ap()

    xT = dram("xT", [128, NKC, TLOC], F32)
    oT = dram("oT", [128, 64, TLOC], BF16)
    w_out = dram("w_out", [8192, D], F32)
    w_up = dram("w_up", [2, D, DFF], F32)
    w_down = dram("w_down", [2, DFF, D], F32)
    pool_w = dram("pool_w", [4, 1024, 1024], F32)
    cvec = dram("cvec", [128, 5, NKC], F32)
    corr = dram("corr", [128, 2, 4, 32], F32)
    outT = dram("outT", [128, NKC, TLOC], F32, kind="ExternalOutput")

    with es:
        S = Sched(nc, es)

        def sb(name, shape, dt):
            return es.enter_context(nc.sbuf_tensor(name, shape, dt))

        h = sb("h", [128, NKC, TT], F32)
        opb = sb("opb", [128, NKC, TT], BF16)
        slots = [sb(f"slot{i}", [128, 8192], BF16) for i in range(NSLOT)]
        r2 = [sb(f"r2_{i}", [128, TT], BF16) for i in range(4)]
        rl = [sb(f"rl{i}", [128, TT], F32) for i in range(2)]
        pscr = [sb(f"pscr{i}", [128, 16 + TT], F32) for i in range(3)]
        rstd = sb("rstd", [128, TT], F32)
        sd = sb("sd", [128, TT], F32)
        carry = sb("carry", [128, NKC, 16], F32)
        cv = sb("cv", [128, 5, NKC], F32)
        corr_sb = sb("corr_sb", [128, 2, 4, 32], F32)
        ones = sb("ones", [128, 128], F32)
        epsb = sb("epsb", [128, 1], F32)
        PA = [es.enter_context(nc.psum_tensor(f"PA{i}", [128, 1024], F32)) for i in range(2)]
        PB = [es.enter_context(nc.psum_tensor(f"PB{i}", [128, 1024], F32)) for i in range(2)]

        def halves(ap2d):
            n = ap2d.shape[-1]
            return ap2d.rearrange("p (b n) -> p b n", b=2)[:, :, 0:HALF]

        S.dma("sp", lambda e: e.dma_start(out=cv[:], in_=cvec[:]), "cv", writes=["cv"])
        S.dma("sp", lambda e: e.dma_start(out=corr_sb[:], in_=corr[:]), "corr", writes=["corr"])
        S.op("pool", lambda e: e.memset(ones[:], 1.0), writes=["ones"])
        S.op("pool", lambda e: e.memset(epsb[:], EPS), writes=["epsb"])
        S.op("pool", lambda e: e.memset(carry[:], 0.0), writes=["carry"])

        slot_ctr = [0]

        def load_w(src_ap, shape3):
            i = slot_ctr[0] % NSLOT
            slot_ctr[0] += 1
            a, b = shape3
            dst = slots[i][:, 0:a * b].rearrange("p (a b) -> p a b", a=a)
            S.dma("pool", lambda e, dst=dst, src=src_ap: e.dma_start(out=dst, in_=src), ("slot", i),
                  writes=[("slot", i)])
            return i, dst

        w_out_v = w_out.rearrange("(kc p) n -> p kc n", p=128)
        w_up_v = [w_up[l].rearrange("(kc p) n -> p kc n", p=128) for l in range(2)]
        w_down_v = [w_down[l].rearrange("(fc p) n -> p fc n", p=128) for l in range(2)]
        pool_w_v = [pool_w[g].rearrange("(kc p) n -> p kc n", p=128) for g in range(4)]

        def mm(out, lhsT, rhs, start, stop, reads, writes, inc):
            S.op("pe", lambda e: e.matmul(out, lhsT=lhsT, rhs=rhs, start=start, stop=stop),
                 reads=reads, writes=writes, inc=inc)

        def rms_rstd():
            for kc in range(NKC):
                sq = rl[kc % 2]
                S.op("act", lambda e, sq=sq, kc=kc: e.activation(out=sq[:], in_=h[:, kc, :], func=AF.Square),
                     reads=[("h", kc)], writes=[("rl", kc % 2)])
                for hf in range(2):
                    mm(PA[0][:, hf * 512:hf * 512 + HALF], ones[:], sq[:, hf * HALF:(hf + 1) * HALF],
                       kc == 0, kc == NKC - 1, [("rl", kc % 2), "ones"], [("PA", 0)], hf == 1)
            S.op("act", lambda e: e.activation(out=halves(sd[:]), in_=halves(PA[0][:]), func=AF.Sqrt,
                                               bias=epsb[:], scale=1.0 / D),
                 reads=[("PA", 0), "epsb"], writes=["sd"])
            S.op("dve", lambda e: e.reciprocal(out=rstd[:], in_=sd[:]), reads=["sd"], writes=["rstd"])

        def norm_to_opb(gidx):
            rms_rstd()
            for kc in range(NKC):
                S.op("dve", lambda e, kc=kc: e.scalar_tensor_tensor(
                    out=opb[:, kc, :], in0=h[:, kc, :], scalar=cv[:, gidx, kc:kc + 1], in1=rstd[:],
                    op0=ALU.mult, op1=ALU.mult),
                    reads=[("h", kc), "cv", "rstd"], writes=[("opb", kc)])

        def mlp(layer, gidx):
            norm_to_opb(gidx)
            NFG = DFF // 256
            st = {}

            def up(fg):
                iu, su = load_w(w_up_v[layer][:, :, fg * 256:(fg + 1) * 256], (NKC, 256))
                idn, sdn = load_w(w_down_v[layer][:, fg * 2:(fg + 1) * 2, :], (2, D))
                st[fg] = (idn, sdn)
                for j in range(2):
                    n = fg * 2 + j
                    pa = PA[n % 2]
                    for hf in range(2):
                        for kc in range(NKC):
                            mm(pa[:, hf * 512:hf * 512 + HALF], su[:, kc, j * 128:(j + 1) * 128],
                               opb[:, kc, hf * HALF:(hf + 1) * HALF], kc == 0, kc == NKC - 1,
                               [("slot", iu), ("opb", kc)], [("PA", n % 2)], hf == 1 and kc == NKC - 1)
                    rr = rl[n % 2]
                    S.op("act", lambda e, rr=rr, pa=pa: e.activation(out=halves(rr[:]), in_=halves(pa[:]), func=AF.Relu),
                         reads=[("PA", n % 2)], writes=[("rl", n % 2)])
                    rb = r2[n % 4]
                    S.op("pool", lambda e, rr=rr, rb=rb: e.tensor_tensor(out=rb[:], in0=rr[:], in1=rr[:], op=ALU.mult),
                         reads=[("rl", n % 2)], writes=[("r2", n % 4)])

            def down(fg):
                idn, sdn = st.pop(fg)
                for db in range(NKC):
                    pb = PB[db % 2]
                    for hf in range(2):
                        for j in range(2):
                            n = fg * 2 + j
                            mm(pb[:, hf * 512:hf * 512 + HALF], sdn[:, j, db * 128:(db + 1) * 128],
                               r2[n % 4][:, hf * HALF:(hf + 1) * HALF], j == 0, j == 1,
                               [("slot", idn), ("r2", n % 4)], [("PB", db % 2)], hf == 1 and j == 1)
                    S.op("dve", lambda e, pb=pb, db=db: e.tensor_tensor(
                        out=halves(h[:, db, :]), in0=halves(pb[:]), in1=halves(h[:, db, :]), op=ALU.add),
                        reads=[("PB", db % 2), ("h", db)], writes=[("h", db)])

            up(0)
            for fg in range(NFG):
                if fg + 1 < NFG:
                    up(fg + 1)
                down(fg)

        for t in range(TLOC // TT):
            t0 = t * TT
            S.dma("sp", lambda e, t0=t0: e.dma_start(out=h[:], in_=xT[:, :, t0:t0 + TT]), "h",
                  writes=[("h", kc) for kc in range(NKC)])
            for kh in range(2):
                S.dma("sp", lambda e, t0=t0, kh=kh: e.dma_start(out=opb[:], in_=oT[:, kh * 32:(kh + 1) * 32, t0:t0 + TT]),
                      "opb", writes=[("opb", kc) for kc in range(NKC)])
                for dg in range(16):
                    iw, sw = load_w(w_out_v[:, kh * 32:(kh + 1) * 32, dg * 256:(dg + 1) * 256], (NKC, 256))
                    for j in range(2):
                        db = dg * 2 + j
                        pb = PB[db % 2]
                        for hf in range(2):
                            for kc in range(NKC):
                                mm(pb[:, hf * 512:hf * 512 + HALF], sw[:, kc, j * 128:(j + 1) * 128],
                                   opb[:, kc, hf * HALF:(hf + 1) * HALF], kc == 0, kc == NKC - 1,
                                   [("slot", iw), ("opb", kc)], [("PB", db % 2)], hf == 1 and kc == NKC - 1)
                        S.op("dve", lambda e, pb=pb, db=db: e.tensor_tensor(
                            out=halves(h[:, db, :]), in0=halves(pb[:]), in1=halves(h[:, db, :]), op=ALU.add),
                            reads=[("PB", db % 2), ("h", db)], writes=[("h", db)])
            mlp(0, 0)
            rms_rstd()
            for kc in range(NKC):
                gi = kc // 8
                win = 2 << gi
                xb = pscr[kc % 3]
                S.op("pool", lambda e, xb=xb, kc=kc: e.tensor_copy(out=xb[:, 0:16], in_=carry[:, kc, :]),
                     reads=["carry"], writes=[("pscr", kc % 3)])
                S.op("dve", lambda e, xb=xb, kc=kc: e.scalar_tensor_tensor(
                    out=xb[:, 16:16 + TT], in0=h[:, kc, :], scalar=cv[:, 1, kc:kc + 1], in1=rstd[:],
                    op0=ALU.mult, op1=ALU.mult),
                    reads=[("h", kc), "cv", "rstd"], writes=[("pscr", kc % 3)])
                S.op("pool", lambda e, xb=xb, kc=kc: e.tensor_copy(out=carry[:, kc, :], in_=xb[:, TT:TT + 16]),
                     reads=[("pscr", kc % 3)], writes=["carry"])
                wa = pscr[(kc + 1) % 3]
                src = xb
                sh = 1
                first = True
                cur = None
                bufs = [pscr[(kc + 1) % 3], pscr[(kc + 2) % 3]]
                bkeys = [("pscr", (kc + 1) % 3), ("pscr", (kc + 2) % 3)]
                srckey = ("pscr", kc % 3)
                bi = 0
                lo_acc = 0
                while sh < win:
                    dst = bufs[bi]
                    dkey = bkeys[bi]
                    lo = lo_acc + sh
                    lo_acc = lo
                    S.op("dve", lambda e, dst=dst, src=src, sh=sh, lo=lo: e.tensor_tensor(
                        out=dst[:, lo:16 + TT], in0=src[:, lo:16 + TT], in1=src[:, lo - sh:16 + TT - sh], op=ALU.add),
                        reads=[srckey], writes=[dkey])
                    src = dst
                    srckey = dkey
                    bi ^= 1
                    sh *= 2
                ws = src
                S.op("pool", lambda e, ws=ws, gi=gi, t=t: e.tensor_tensor(
                    out=ws[:, 16:48], in0=ws[:, 16:48], in1=corr_sb[:, t, gi, :], op=ALU.mult),
                    reads=[srckey, "corr"], writes=[srckey])
                S.op("dve", lambda e, ws=ws, xb=xb, kc=kc, win=win: e.scalar_tensor_tensor(
                    out=opb[:, kc, :], in0=ws[:, 16:16 + TT], scalar=1.0 / win, in1=xb[:, 16:16 + TT],
                    op0=ALU.mult, op1=ALU.subtract),
                    reads=[srckey, ("pscr", kc % 3)], writes=[("opb", kc)])
            for gi in range(4):
                ip, sp_ = load_w(pool_w_v[gi], (8, 1024))
                for j in range(8):
                    db = gi * 8 + j
                    pb = PB[db % 2]
                    for hf in range(2):
                        for kk in range(8):
                            kc = gi * 8 + kk
                            mm(pb[:, hf * 512:hf * 512 + HALF], sp_[:, kk, j * 128:(j + 1) * 128],
                               opb[:, kc, hf * HALF:(hf + 1) * HALF], kk == 0, kk == 7,
                               [("slot", ip), ("opb", kc)], [("PB", db % 2)], hf == 1 and kk == 7)
                    S.op("dve", lambda e, pb=pb, db=db: e.scalar_tensor_tensor(
                        out=halves(h[:, db, :]), in0=halves(pb[:]), scalar=cv[:, 4, db:db + 1], in1=halves(h[:, db, :]),
                        op0=ALU.mult, op1=ALU.add),
                        reads=[("PB", db % 2), ("h", db), "cv"], writes=[("h", db)])
            mlp(1, 2)
            rms_rstd()
            for kc in range(NKC):
                ob = pscr[kc % 3]
                S.op("dve", lambda e, ob=ob, kc=kc: e.scalar_tensor_tensor(
                    out=ob[:, 0:TT], in0=h[:, kc, :], scalar=cv[:, 3, kc:kc + 1], in1=rstd[:],
                    op0=ALU.mult, op1=ALU.mult),
                    reads=[("h", kc), "cv", "rstd"], writes=[("pscr", kc % 3)])
                S.dma("sp", lambda e, ob=ob, kc=kc, t0=t0: e.dma_start(out=outT[:, kc, t0:t0 + TT], in_=ob[:, 0:TT]),
                      ("ost", kc % 3), reads=[("pscr", kc % 3)], is_out=True)
        S.finish()
        block = es.enter_context(nc.Block())
        S.emit(block)
    return nc


NHL = 32
NKL = 16
HG = 8
NCH = LPAD // 64


def build_stageA(do_phase2=True, do_phase1=True, stop_after=99, nhg=NHL // HG, nch=NCH):
    nc = bass.Bass("TRN2", target_bir_lowering=False)
    es = ExitStack()

    def dram(name, shape, dt, kind="ExternalInput"):
        return nc.dram_tensor(name, shape, dt, kind=kind).ap()

    xT = dram("xT", [128, NKC, LPAD], F32)
    wq = dram("wqkvz", [D, 96 * 128], F32)
    wba_d = dram("wba", [D, 64], F32)
    cw_d = dram("convw", [128, 64, 4], F32)
    gv_d = dram("gvec", [128, NKC], F32)
    al_d = dram("alog", [64, NHL], F32)
    dtb_d = dram("dtb", [64, NHL], F32)
    on_d = dram("onorm", [128, 1], F32)
    mk_d = dram("masks", [64, 3, HG, 64], F32)
    id_d = dram("ident", [128, 128], BF16)
    idf_d = dram("identf", [64, HG, 64], F32)
    oT = dram("oT", [128, NHL, LPAD], BF16, kind="ExternalOutput")
    IK = "ExternalOutput"
    QS = dram("QS", [128, NKL, LPAD], BF16, kind=IK)
    KS = dram("KS", [128, NKL, LPAD], BF16, kind=IK)
    VS = dram("VS", [128, NHL, LPAD], BF16, kind=IK)
    ZS = dram("ZS", [128, NHL, LPAD], BF16, kind=IK)
    BA = dram("BA", [LPAD, 64], F32, kind=IK)

    with es:
        S = Sched(nc, es)

        def sb(name, shape, dt):
            return es.enter_context(nc.sbuf_tensor(name, shape, dt))

        def ps(name, shape, dt):
            return es.enter_context(nc.psum_tensor(name, shape, dt))

        def mm(out, lhsT, rhs, start, stop, reads, writes, inc):
            S.op("pe", lambda e: e.matmul(out, lhsT=lhsT, rhs=rhs, start=start, stop=stop),
                 reads=reads, writes=writes, inc=inc)

        def tr(out, in_, idn, reads, writes, inc):
            S.op("pe", lambda e: e.matmul(out, lhsT=in_, rhs=idn, start=True, stop=True), reads=reads, writes=writes, inc=inc)

        def act(out, in_, func, reads, writes, scale=1.0, bias=None):
            if bias is None:
                S.op("act", lambda e: e.activation(out=out, in_=in_, func=func, scale=scale), reads=reads, writes=writes)
            else:
                S.op("act", lambda e: e.activation(out=out, in_=in_, func=func, scale=scale, bias=bias),
                     reads=reads, writes=writes)

        def ts(eng, out, in0, s1, op0, reads, writes, s2=None, op1=None):
            if op1 is None:
                S.op(eng, lambda e: e.tensor_scalar(out=out, in0=in0, scalar1=s1, scalar2=None, op0=op0),
                     reads=reads, writes=writes)
            else:
                S.op(eng, lambda e: e.tensor_scalar(out=out, in0=in0, scalar1=s1, scalar2=s2, op0=op0, op1=op1),
                     reads=reads, writes=writes)

        def tt(eng, out, in0, in1, op, reads, writes):
            S.op(eng, lambda e: e.tensor_tensor(out=out, in0=in0, in1=in1, op=op), reads=reads, writes=writes)

        def stt(out, in0, scalar, in1, op0, op1, reads, writes):
            S.op("dve", lambda e: e.scalar_tensor_tensor(out=out, in0=in0, scalar=scalar, in1=in1, op0=op0, op1=op1),
                 reads=reads, writes=writes)

        def cp(eng, out, in_, reads, writes):
            S.op(eng, lambda e: e.tensor_copy(out=out, in_=in_), reads=reads, writes=writes)

        def halves(ap2d):
            return ap2d.rearrange("p (b n) -> p b n", b=2)[:, :, 0:HALF]

        cw = sb("cw", [128, 64, 4], F32)
        gv = sb("gv", [128, NKC], F32)
        alog = sb("alog_sb", [64, NHL], F32)
        dtb = sb("dtb_sb", [64, NHL], F32)
        nA = sb("nA", [64, NHL], F32)
        onrm = sb("onrm", [128, 1], F32)
        masks = sb("masks_sb", [64, 3, HG, 64], F32)
        ident = sb("ident_sb", [128, 128], BF16)
        identf = sb("identf_sb", [64, HG, 64], F32)
        ones = sb("ones", [128, 128], F32)
        epsb = sb("epsb", [128, 1], F32)
        oneb = sb("oneb", [128, 1], F32)
        wba = sb("wba_sb", [128, NKC, 64], BF16)
        for i, (dst, src) in enumerate([(cw, cw_d), (gv, gv_d), (alog, al_d), (dtb, dtb_d), (onrm, on_d),
                                        (masks, mk_d), (ident, id_d), (identf, idf_d)]):
            S.dma("sp", lambda e, dst=dst, src=src: e.dma_start(out=dst[:], in_=src[:]), ("c", i), writes=[("c", i)])
        CK = [("c", i) for i in range(8)]
        K_cw, K_gv, K_al, K_dtb, K_on, K_mk, K_id, K_idf = CK
        S.dma("pool", lambda e: e.dma_start(out=wba[:], in_=wba_d.rearrange("(kc p) n -> p kc n", p=128)), "wba",
              writes=["wba"])
        S.op("pool", lambda e: e.memset(ones[:], 1.0), writes=["ones"])
        S.op("pool", lambda e: e.memset(epsb[:], EPS), writes=["epsb"])
        S.op("pool", lambda e: e.memset(oneb[:], 1.0), writes=["oneb"])
        act(nA[:], alog[:], AF.Exp, [K_al], ["nA"])
        ts("dve", nA[:], nA[:], -1.0, ALU.mult, ["nA"], ["nA"])
        MUI = masks[:, 0]
        MLS = masks[:, 2]

        PA = [ps(f"PA{i}", [128, 1024], F32) for i in range(2)]
        PC = [ps(f"PC{i}", [128, 512], F32) for i in range(2)]
        PT = [ps(f"PT{i}", [128, 512], F32) for i in range(2)]

        hn = sb("hn", [128, NKC, TT], BF16)
        slots = [sb(f"slot{i}", [128, 8192], BF16) for i in range(3)]
        xbuf = [sb(f"xbuf{i}", [128, TT], F32) for i in range(2)]
        sqb = sb("sqb", [128, TT], F32)
        rstd = sb("rstd", [128, TT], F32)
        sd = sb("sd", [128, TT], F32)
        pre = [sb(f"pre{i}", [128, TT + 3], F32) for i in range(2)]
        acc = sb("acc", [128, TT], F32)
        sil = sb("sil", [128, TT], F32)
        outb = [sb(f"outb{i}", [128, TT], BF16) for i in range(2)]
        carryc = sb("carryc", [128, 64, 3], F32)
        babuf = [sb(f"babuf{i}", [66, 64], F32) for i in range(2)]
        S.op("pool", lambda e: e.memset(carryc[:], 0.0), writes=["carryc"])
        KPA = [[("PA", 0)], [("PAq", 1), ("PAp", 1)]]
        wq_v = wq.rearrange("(kc p) n -> p kc n", p=128)
        slot_ctr = [0]
        ob_ctr = [0]

        def store_out(dst_dram, blk_idx, t0, key):
            i = ob_ctr[0] % 2
            ob_ctr[0] += 1
            return i

        for t in range(LPAD // TT if do_phase1 else 0):
            t0 = t * TT
            for kc in range(NKC):
                xb = xbuf[kc % 2]
                S.dma("sp", lambda e, xb=xb, kc=kc, t0=t0: e.dma_start(out=xb[:], in_=xT[:, kc, t0:t0 + TT]),
                      ("xbuf", kc % 2), writes=[("xbuf", kc % 2)])
                act(sqb[:], xb[:], AF.Square, [("xbuf", kc % 2)], ["sqb"])
                for hf in range(2):
                    mm(PA[0][:, hf * 512:hf * 512 + HALF], ones[:], sqb[:, hf * HALF:(hf + 1) * HALF],
                       kc == 0, kc == NKC - 1, ["sqb", "ones"], [("PA", 0)], hf == 1)
            act(halves(sd[:]), halves(PA[0][:]), AF.Sqrt, [("PA", 0), "epsb"], ["sd"], scale=1.0 / D, bias=epsb[:])
            S.op("dve", lambda e: e.reciprocal(out=rstd[:], in_=sd[:]), reads=["sd"], writes=["rstd"])
            for kc in range(NKC):
                xb = xbuf[kc % 2]
                S.dma("sp", lambda e, xb=xb, kc=kc, t0=t0: e.dma_start(out=xb[:], in_=xT[:, kc, t0:t0 + TT]),
                      ("xbuf", kc % 2), writes=[("xbuf", kc % 2)])
                stt(hn[:, kc, :], xb[:], gv[:, kc:kc + 1], rstd[:], ALU.mult, ALU.mult,
                    [("xbuf", kc % 2), K_gv, "rstd"], [("hn", kc)])
            for sbk in range(TT // 66):
                pc = PC[sbk % 2]
                for kc in range(NKC):
                    mm(pc[0:66, 0:64], hn[:, kc, sbk * 66:(sbk + 1) * 66], wba[:, kc, :], kc == 0, kc == NKC - 1,
                       [("hn", kc), "wba"], [("PC", sbk % 2)], kc == NKC - 1)
                bb = babuf[sbk % 2]
                act(bb[:], pc[0:66, 0:64], AF.Copy, [("PC", sbk % 2)], [("babuf", sbk % 2)])
                S.dma("sp", lambda e, bb=bb, r0=t0 + sbk * 66: e.dma_start(out=BA[r0:r0 + 66, :], in_=bb[:]),
                      ("babuf", sbk % 2), reads=[("babuf", sbk % 2)], writes=["BA"], is_out=True)
            for pair in range(48):
                si = slot_ctr[0] % 3
                slot_ctr[0] += 1
                sw = slots[si][:, :].rearrange("p (a b) -> p a b", a=NKC)
                S.dma("pool", lambda e, sw=sw, pair=pair: e.dma_start(out=sw, in_=wq_v[:, :, pair * 256:(pair + 1) * 256]),
                      ("slot", si), writes=[("slot", si)])
                for j in range(2):
                    blk = pair * 2 + j
                    pa = PA[blk % 2]
                    for hf in range(2):
                        for kc in range(NKC):
                            mm(pa[:, hf * 512:hf * 512 + HALF], sw[:, kc, j * 128:(j + 1) * 128],
                               hn[:, kc, hf * HALF:(hf + 1) * HALF], kc == 0, kc == NKC - 1,
                               [("slot", si), ("hn", kc)], KPA[blk % 2], hf == 1 and kc == NKC - 1)
                    oi = ob_ctr[0] % 2
                    ob_ctr[0] += 1
                    ob = outb[oi]
                    if blk < 64:
                        pr = pre[blk % 2]
                        kpr = ("pre", blk % 2)
                        cp("pool", pr[:, 0:3], carryc[:, blk, :], ["carryc"], [kpr])
                        act(halves(pr[:, 3:3 + TT]), halves(pa[:]), AF.Copy, KPA[blk % 2], [kpr])
                        cp("pool", carryc[:, blk, :], pr[:, TT:TT + 3], [kpr], ["carryc"])
                        ts("dve", acc[:], pr[:, 0:TT], cw[:, blk, 0:1], ALU.mult, [kpr, K_cw], ["acc"])
                        for tap in range(1, 4):
                            stt(acc[:], pr[:, tap:tap + TT], cw[:, blk, tap:tap + 1], acc[:], ALU.mult, ALU.add,
                                [kpr, K_cw, "acc"], ["acc"])
                        if blk < 32:
                            act(sil[:], acc[:], AF.Silu, ["acc"], ["sil"])
                            act(sqb[:], sil[:], AF.Square, ["sil"], ["sqb"])
                            pl = PA[blk % 2]
                            for hf in range(2):
                                mm(pl[:, hf * 512:hf * 512 + HALF], ones[:], sqb[:, hf * HALF:(hf + 1) * HALF], True, True,
                                   ["sqb", "ones"], KPA[blk % 2], hf == 1)
                            act(halves(sd[:]), halves(pl[:]), AF.Sqrt, KPA[blk % 2] + ["epsb"], ["sd"], bias=epsb[:])
                            S.op("dve", lambda e: e.reciprocal(out=rstd[:], in_=sd[:]), reads=["sd"], writes=["rstd"])
                            stt(ob[:], sil[:], (128.0 ** -0.5) if blk < 16 else 1.0, rstd[:], ALU.mult, ALU.mult,
                                ["sil", "rstd"], [("outb", oi)])
                            dst = (QS if blk < 16 else KS)[:, blk % 16, t0:t0 + TT]
                            dkey = ("QS" if blk < 16 else "KS")
                        else:
                            act(ob[:], acc[:], AF.Silu, ["acc"], [("outb", oi)])
                            dst = VS[:, blk - 32, t0:t0 + TT]
                            dkey = "VS"
                    else:
                        act(halves(ob[:]), halves(pa[:]), AF.Silu, KPA[blk % 2], [("outb", oi)])
                        dst = ZS[:, blk - 64, t0:t0 + TT]
                        dkey = "ZS"
                    S.dma("sp", lambda e, dst=dst, ob=ob: e.dma_start(out=dst, in_=ob[:]), ("outb", oi),
                          reads=[("outb", oi)], writes=[dkey], is_out=True)

        Sst = sb("Sst", [128, HG, 128], F32)
        Sbf = sb("Sbf", [128, HG, 128], BF16)
        qc = [sb(f"qc{i}", [128, 4, 64], BF16) for i in range(2)]
        kcb = [sb(f"kcb{i}", [128, 4, 64], BF16) for i in range(2)]
        vc = [sb(f"vc{i}", [128, HG, 64], BF16) for i in range(2)]
        zc = [sb(f"zc{i}", [128, HG, 64], BF16) for i in range(2)]
        bac = [sb(f"bac{i}", [64, 64], F32) for i in range(2)]
        sm = {n: sb("sm_" + n, [64, HG], F32) for n in ("e1", "beta", "xa", "ax", "e2", "ln", "g", "gc", "tmp", "kds", "egc", "bg", "r1", "r2")}
        egl = sb("egl", [128, HG], F32)
        LGb = [sb(f"LGb{i}", [64, HG, 64], BF16) for i in range(3)]
        gpc = [sb(f"gpc{i}", [64, HG], BF16) for i in range(3)]
        MUIb = sb("MUIb", [64, 64], BF16)
        onesb = sb("onesb", [128, 128], BF16)
        S.op("pool", lambda e: e.memset(onesb[:], 1.0), writes=["onesb"])
        cp("pool", MUIb[:], MUI[:, 0, :], [K_mk], ["MUIb"])
        DL = sb("DL", [64, HG, 64], F32)
        EU = sb("EU", [64, HG, 64], F32)
        EL = sb("EL", [64, HG, 64], F32)
        Abf = sb("Abf", [64, HG, 64], BF16)
        Pb = [sb(f"Pb{i}", [64, HG, 64], BF16) for i in range(2)]
        Qb = [sb(f"Qb{i}", [64, HG, 64], BF16) for i in range(2)]
        Rf = sb("Rf", [64, HG, 64], F32)
        Rb = sb("Rb", [64, HG, 64], BF16)
        Kbg = sb("Kbg", [64, HG, 128], BF16)
        Kd = sb("Kd", [64, HG, 128], BF16)
        Vb = sb("Vb", [64, HG, 128], BF16)
        WTb = sb("WTb", [128, HG, 64], BF16)
        Usb = sb("Usb", [64, HG, 128], F32)
        Vn = sb("Vn", [64, HG, 128], BF16)
        QKm = sb("QKm", [64, HG, 64], BF16)
        EG = sb("EG", [128, HG, 64], F32)
        QdT = sb("QdT", [128, HG, 64], BF16)
        Oc = sb("Oc", [128, HG, 64], F32)
        Osq = sb("Osq", [128, HG, 64], F32)
        Ord = sb("Ord", [128, HG, 64], F32)
        Ob = [sb(f"Ob{i}", [128, HG, 64], BF16) for i in range(2)]

        def v3(ap2d, a):
            return ap2d.rearrange("p (a b) -> p a b", a=a)

        def p4(ap3):
            return ap3.rearrange("p (a b) n -> p a b n", b=2)

        for hg in range(nhg if do_phase2 else 0):
            hb = hg * HG
            S.op("pool", lambda e: e.memset(Sst[:], 0.0), reads=[], writes=["Sst"])
            S.op("pool", lambda e: e.memset(Sbf[:], 0.0), reads=[], writes=["Sbf"])
            def load_chunk(c, hg=hg, hb=hb):
                c0 = c * 64
                bi = c % 2
                S.dma("sp", lambda e: e.dma_start(out=qc[bi][:], in_=QS[:, hg * 4:hg * 4 + 4, c0:c0 + 64]),
                      ("qc", bi), reads=["QS"], writes=[("qc", bi)])
                S.dma("sp", lambda e: e.dma_start(out=kcb[bi][:], in_=KS[:, hg * 4:hg * 4 + 4, c0:c0 + 64]),
                      ("kcb", bi), reads=["KS"], writes=[("kcb", bi)])
                S.dma("sp", lambda e: e.dma_start(out=vc[bi][:], in_=VS[:, hb:hb + HG, c0:c0 + 64]),
                      ("vc", bi), reads=["VS"], writes=[("vc", bi)])
                S.dma("sp", lambda e: e.dma_start(out=zc[bi][:], in_=ZS[:, hb:hb + HG, c0:c0 + 64]),
                      ("zc", bi), reads=["ZS"], writes=[("zc", bi)])
                S.dma("sp", lambda e: e.dma_start(out=bac[bi][:], in_=BA[c0:c0 + 64, :]),
                      ("bac", bi), reads=["BA"], writes=[("bac", bi)])
            load_chunk(0)
            for c in range(nch):
                c0 = c * 64
                bi = c % 2
                if c + 1 < nch:
                    load_chunk(c + 1)
                q_, k_, v_, z_, ba_ = qc[bi], kcb[bi], vc[bi], zc[bi], bac[bi]
                Kq, Kk, Kv, Kz, Kba = ("qc", bi), ("kcb", bi), ("vc", bi), ("zc", bi), ("bac", bi)
                act(sm["e1"][:], ba_[:, hb:hb + HG], AF.Exp, [Kba], ["e1"], scale=-1.0)
                ts("dve", sm["e1"][:], sm["e1"][:], 1.0, ALU.add, ["e1"], ["e1"])
                S.op("dve", lambda e: e.reciprocal(out=sm["beta"][:], in_=sm["e1"][:]), reads=["e1"], writes=["beta"])
                tt("dve", sm["xa"][:], ba_[:, 32 + hb:32 + hb + HG], dtb[:, hb:hb + HG], ALU.add, [Kba, K_dtb], ["xa"])
                stt(sm["ax"][:], sm["xa"][:], -1.0, sm["xa"][:], ALU.mult, ALU.max, ["xa"], ["ax"])
                act(sm["e2"][:], sm["ax"][:], AF.Exp, ["ax"], ["e2"], scale=-1.0)
                act(sm["ln"][:], sm["e2"][:], AF.Ln, ["e2", "oneb"], ["ln"], bias=oneb[0:64, :])
                stt(sm["g"][:], sm["xa"][:], 0.0, sm["ln"][:], ALU.max, ALU.add, ["xa", "ln"], ["g"])
                tt("dve", sm["g"][:], sm["g"][:], nA[:, hb:hb + HG], ALU.mult, ["g", "nA"], ["g"])
                mm(PC[0][0:64, 0:HG], MUI[:, 0, :], sm["g"][:], True, True, [K_mk, "g"], [("PC", 0)], False)
                mm(PC[0][:, 64:64 + HG], ones[0:64, :], sm["g"][:], True, True, ["ones", "g"], [("PC", 0)], True)
                cp("dve", sm["gc"][:], PC[0][0:64, 0:HG], [("PC", 0)], ["gc"])
                act(egl[:], PC[0][:, 64:64 + HG], AF.Exp, [("PC", 0)], ["egl"])
                tt("dve", sm["tmp"][:], PC[0][0:64, 64:64 + HG], sm["gc"][:], ALU.subtract, [("PC", 0), "gc"], ["tmp"])
                act(sm["kds"][:], sm["tmp"][:], AF.Exp, ["tmp"], ["kds"])
                act(sm["egc"][:], sm["gc"][:], AF.Exp, ["gc"], ["egc"])
                tt("dve", sm["bg"][:], sm["egc"][:], sm["beta"][:], ALU.mult, ["egc", "beta"], ["bg"])
                if stop_after < 1.1:
                    continue
                pkq = v3(PC[1][0:64, :], 8)
                for kh in range(4):
                    mm(pkq[:, kh, :], k_[:, kh, :], k_[:, kh, :], True, True, [Kk], [("PC", 1)], False)
                    mm(pkq[:, 4 + kh, :], k_[:, kh, :], q_[:, kh, :], True, True, [Kk, Kq], [("PC", 1)], kh == 3)
                pkt = v3(PT[0][0:64, :], 4)
                pvt = v3(PA[1][0:64, :], 8)
                for kh in range(4):
                    tr(pkt[:, kh, :], k_[:, kh, :], ident[:], [Kk, K_id], [("PT", 0)], kh == 3)
                for j in range(HG):
                    tr(pvt[:, j, :], v_[:, j, :], ident[:], [Kv, K_id], [("PAq", 1), ("PAp", 1)], j == HG - 1)
                if stop_after < 1.25:
                    continue
                cp("dve", gpc[0][:], sm["g"][:], ["g"], ["gp0"])
                tt("dve", sm["r1"][:], sm["g"][:], gpc[0][:], ALU.subtract, ["g", "gp0"], ["r1"])
                cp("dve", gpc[1][:], sm["r1"][:], ["r1"], ["gp1"])
                tt("dve", sm["r2"][:], sm["r1"][:], gpc[1][:], ALU.subtract, ["r1", "gp1"], ["r2"])
                cp("dve", gpc[2][:], sm["r2"][:], ["r2"], ["gp2"])
                for k3 in range(3):
                    tt("dve", LGb[k3][:], MUI, gpc[k3][:].unsqueeze(2).to_broadcast([64, HG, 64]), ALU.mult,
                       [K_mk, "gp%d" % k3], [("LGb", k3)])
                pgr = v3(PA[0][:, 0:512], 8)
                for j in range(HG):
                    for k3 in range(3):
                        mm(pgr[:, j, :], onesb[0:64, :], LGb[k3][:, j, :], k3 == 0, k3 == 2, ["onesb", ("LGb", k3)],
                           [("PA", 0)], j == HG - 1 and k3 == 2)
                if stop_after < 1.5:
                    continue
                tt("dve", DL[:], pgr[0:64], sm["gc"][:].unsqueeze(2).to_broadcast([64, HG, 64]), ALU.subtract,
                   [("PA", 0), "gc"], ["DL"])
                act(EG[:], pgr[:], AF.Exp, [("PA", 0)], ["EG"])
                if stop_after < 1.75:
                    continue
                ts("dve", EU[:], DL[:], 0.0, ALU.min, ["DL"], ["EU"])
                act(EU[:], EU[:], AF.Exp, ["EU"], ["EU"])
                tt("pool", EU[:], EU[:], MUI, ALU.mult, ["EU", K_mk], ["EU"])
                ts("dve", EL[:], DL[:], -1.0, ALU.mult, ["DL"], ["EL"], s2=0.0, op1=ALU.min)
                act(EL[:], EL[:], AF.Exp, ["EL"], ["EL"])
                tt("pool", EL[:], EL[:], MLS, ALU.mult, ["EL", K_mk], ["EL"])
                if stop_after < 3:
                    continue
                tt("dve", EL[:], EL[:], sm["beta"][:].unsqueeze(2).to_broadcast([64, HG, 64]), ALU.mult, ["EL", "beta"], ["EL"])
                tt("dve", p4(Abf[:]), pkq[:, 0:4, :].unsqueeze(2).to_broadcast([64, 4, 2, 64]), p4(EL[:]), ALU.mult,
                   [("PC", 1), "EL"], ["Abf"])
                tt("dve", p4(QKm[:]), pkq[:, 4:8, :].unsqueeze(2).to_broadcast([64, 4, 2, 64]), p4(EU[:]), ALU.mult,
                   [("PC", 1), "EU"], ["QKm"])
                pkt4 = pkt.unsqueeze(2).to_broadcast([64, 4, 2, 128])
                tt("dve", p4(Kbg[:]), pkt4, p4(sm["bg"][:].unsqueeze(2).to_broadcast([64, HG, 128])), ALU.mult,
                   [("PT", 0), "bg"], ["Kbg"])
                tt("dve", p4(Kd[:]), pkt4, p4(sm["kds"][:].unsqueeze(2).to_broadcast([64, HG, 128])), ALU.mult,
                   [("PT", 0), "kds"], ["Kd"])
                tt("dve", Vb[:], pvt, sm["beta"][:].unsqueeze(2).to_broadcast([64, HG, 128]), ALU.mult,
                   [("PAq", 1), ("PAp", 1), "beta"], ["Vb"])
                tt("pool", p4(QdT[:]), q_[:].unsqueeze(2).to_broadcast([128, 4, 2, 64]), p4(EG[:]), ALU.mult, [Kq, "EG"], ["QdT"])
                pbt = v3(PT[1][0:64, :], 8)
                for j in range(HG):
                    tr(pbt[:, j, :], Abf[:, j, :], ident[0:64, 0:64], ["Abf", K_id], [("PT", 1)], j == HG - 1)
                cp("dve", Pb[0][:], pbt[:], [("PT", 1)], [("Pb", 0)])
                cp("pool", Qb[0][:], Abf[:], ["Abf"], [("Qb", 0)])
                tt("dve", Rf[:], identf[:], pbt[:], ALU.subtract, [K_idf, ("PT", 1)], ["Rf"])
                cp("pool", Rb[:], Rf[:], ["Rf"], ["Rb"])
                for lv in range(1, 6):
                    pi, po = (lv - 1) % 2, lv % 2
                    pq = v3(PA[1][0:64, 0:512], 8)
                    pp = v3(PA[1][0:64, 512:1024], 8)
                    for j in range(HG):
                        mm(pq[:, j, :], Pb[pi][:, j, :], Qb[pi][:, j, :], True, True, [("Pb", pi), ("Qb", pi)], [("PAq", 1)],
                           j == HG - 1)
                    if lv < 5:
                        for j in range(HG):
                            mm(pp[:, j, :], Qb[pi][:, j, :], Pb[pi][:, j, :], True, True, [("Pb", pi), ("Qb", pi)],
                               [("PAp", 1)], j == HG - 1)
                    cp("dve", Qb[po][:], pq[:], [("PAq", 1)], [("Qb", po)])
                    if lv < 5:
                        act(Pb[po][:], pp[:], AF.Copy, [("PAp", 1)], [("Pb", po)])
                    pr_ = v3(PC[0][0:64, :], 8)
                    for j in range(HG):
                        mm(pr_[:, j, :], Qb[po][:, j, :], Rb[:, j, :], True, True, [("Qb", po), "Rb"], [("PC", 0)], j == HG - 1)
                    tt("dve", Rf[:], Rf[:], pr_[:], ALU.add, ["Rf", ("PC", 0)], ["Rf"])
                    cp("pool", Rb[:], Rf[:], ["Rf"], ["Rb"])
                if stop_after < 4:
                    continue
                pwt = v3(PA[0][:, 0:512], 8)
                for j in range(HG):
                    mm(pwt[:, j, :], Kbg[:, j, :], Rb[:, j, :], True, True, ["Kbg", "Rb"], [("PA", 0)], j == HG - 1)
                act(WTb[:], pwt[:], AF.Copy, [("PA", 0)], ["WTb"])
                pu = v3(PA[1][0:64, :], 8)
                for j in range(HG):
                    mm(pu[:, j, :], Rb[:, j, :], Vb[:, j, :], True, True, ["Rb", "Vb"], [("PAq", 1), ("PAp", 1)], j == HG - 1)
                act(Usb[:], pu[:], AF.Copy, [("PAq", 1), ("PAp", 1)], ["Usb"])
                pws = v3(PA[1][0:64, :], 8)
                for j in range(HG):
                    mm(pws[:, j, :], WTb[:, j, :], Sbf[:, j, :], True, True, ["WTb", "Sbf"], [("PAq", 1), ("PAp", 1)], j == HG - 1)
                tt("dve", Vn[:], Usb[:], pws[:], ALU.subtract, ["Usb", ("PAq", 1), ("PAp", 1)], ["Vn"])
                po_ = v3(PC[1][:, :], 8)
                for j in range(HG):
                    mm(po_[:, j, :], Vn[:, j, :], QKm[:, j, :], True, False, ["Vn", "QKm"], [("PC", 1)], False)
                    mm(po_[:, j, :], Sbf[:, j, :], QdT[:, j, :], False, True, ["Sbf", "QdT"], [("PC", 1)], j == HG - 1)
                pss = v3(PA[0][:, :], 8)
                for j in range(HG):
                    mm(pss[:, j, :], Kd[:, j, :], Vn[:, j, :], True, True, ["Kd", "Vn"], [("PA", 0)], j == HG - 1)
                for j in range(HG):
                    stt(Sst[:, j, :], Sst[:, j, :], egl[:, j:j + 1], pss[:, j, :], ALU.mult, ALU.add,
                        ["Sst", "egl", ("PA", 0)], ["Sst"])
                cp("pool", Sbf[:], Sst[:], ["Sst"], ["Sbf"])
                act(Oc[:], po_[:], AF.Copy, [("PC", 1)], ["Oc"])
                act(Osq[:], Oc[:], AF.Square, ["Oc"], ["Osq"])
                pn = PC[0][:, :]
                mm(pn, ones[:], Osq[:].rearrange("p a b -> p (a b)"), True, True, ["ones", "Osq"], [("PC", 0)], True)
                act(Osq[:].rearrange("p a b -> p (a b)"), pn, AF.Sqrt, [("PC", 0), "epsb"], ["Osq"], scale=1.0 / 128, bias=epsb[:])
                S.op("dve", lambda e: e.reciprocal(out=Ord[:], in_=Osq[:]), reads=["Osq"], writes=["Ord"])
                stt(Oc[:].rearrange("p a b -> p (a b)"), Oc[:].rearrange("p a b -> p (a b)"), onrm[:, 0:1],
                    Ord[:].rearrange("p a b -> p (a b)"), ALU.mult, ALU.mult, ["Oc", K_on, "Ord"], ["Oc"])
                tt("dve", Ob[bi][:], Oc[:], z_[:], ALU.mult, ["Oc", Kz], [("Ob", bi)])
                S.dma("sp", lambda e, bi=bi, hb=hb, c0=c0: e.dma_start(out=oT[:, hb:hb + HG, c0:c0 + 64], in_=Ob[bi][:]),
                      ("Ob", bi), reads=[("Ob", bi)], is_out=True)
        S.finish()
        block = es.enter_context(nc.Block())
        S.emit(block)
    return nc


def _fm(a2d):
    t, f = a2d.shape
    return np.ascontiguousarray(a2d.T.reshape(f // 128, 128, t).transpose(1, 0, 2))


def _vec(v):
    return np.ascontiguousarray(v.reshape(-1, 128).T)


def _seq_pad(inp, b):
    return np.concatenate([np.zeros((LPAD - SEQ - NMETA, D), np.float32),
                           np.asarray(inp["meta_tokens"], np.float32),
                           np.asarray(inp["x"][b], np.float32)], axis=0)


def _corr_table(th):
    c = np.ones((128, 2, 4, 32), np.float32)
    if th == 0:
        for gi, win in enumerate((2, 4, 8, 16)):
            for i in range(16, 32):
                pos = i - 15
                c[:, 0, gi, i] = float(win) / float(min(pos, win))
    return c


def stageB_inputs(inp, og_pad_bf16):
    cvec = np.stack([_vec(inp["mlp_norm"][0]), _vec(inp["mix_norm"][1]), _vec(inp["mlp_norm"][1]),
                     _vec(inp["final_norm"]), _vec(inp["pool_scale"][0])], axis=1).astype(np.float32)
    shared = {"w_out": np.asarray(inp["dn_w_out"][0]), "w_up": np.asarray(inp["w_up"]),
              "w_down": np.asarray(inp["w_down"]), "pool_w": np.asarray(inp["pool_w"][0]),
              "cvec": np.ascontiguousarray(cvec)}
    maps = []
    for c in range(8):
        b, th = c // 2, c % 2
        s0 = 32 if th == 0 else LPAD - TLOC
        sp = _seq_pad(inp, b)[s0:s0 + TLOC]
        m = dict(shared)
        m["xT"] = _fm(sp)
        m["oT"] = _fm(og_pad_bf16[b, s0:s0 + TLOC])
        m["corr"] = _corr_table(th)
        maps.append(m)
    return maps


def stageB_gather(outs):
    res = np.zeros((4, SEQ, D), np.float32)
    for c in range(8):
        b, th = c // 2, c % 2
        o = outs[c]
        tok = o.transpose(2, 1, 0).reshape(TLOC, D)
        res[b, th * 1024:(th + 1) * 1024] = tok[32:]
    return res


def _masks():
    p = np.arange(64)[:, None]
    n = np.arange(64)[None, :]
    m = np.stack([(p <= n), (p < n), (n < p)], 0).astype(np.float32)
    m = np.broadcast_to(m[:, :, None, :], (3, 64, HG, 64)).transpose(1, 0, 2, 3)
    return np.ascontiguousarray(m)


def stageA_inputs(inp):
    w_in = np.asarray(inp["dn_w_in"][0])
    conv_w = np.asarray(inp["dn_conv_w"][0])
    maps = []
    identf = np.ascontiguousarray(np.broadcast_to(np.eye(64, dtype=np.float32)[:, None, :], (64, HG, 64)))
    shared = {"gvec": _vec(np.asarray(inp["mix_norm"][0], np.float32)),
              "onorm": np.ascontiguousarray(np.asarray(inp["dn_out_norm"][0], np.float32)[:, None]),
              "masks": _masks(), "ident": np.eye(128, dtype=np.float32).astype(ml_dtypes.bfloat16),
              "identf": identf}
    per_hh = {}
    for hh in range(2):
        q0, k0, v0, z0 = 16 * hh * 128, 4096 + 16 * hh * 128, 8192 + 32 * hh * 128, 16384 + 32 * hh * 128
        wqkvz = np.ascontiguousarray(np.concatenate(
            [w_in[:, q0:q0 + 2048], w_in[:, k0:k0 + 2048], w_in[:, v0:v0 + 4096], w_in[:, z0:z0 + 4096]], axis=1))
        b0, a0 = 24576 + 32 * hh, 24576 + 64 + 32 * hh
        wba = np.ascontiguousarray(np.concatenate([w_in[:, b0:b0 + 32], w_in[:, a0:a0 + 32]], axis=1))
        ch = np.concatenate([np.arange(q0, q0 + 2048), np.arange(k0, k0 + 2048), np.arange(v0, v0 + 4096)])
        cw = conv_w[:, ch]
        convw = np.ascontiguousarray(cw.reshape(4, 64, 128).transpose(2, 1, 0))
        alog = np.ascontiguousarray(np.broadcast_to(np.asarray(inp["dn_a_log"][0])[None, 32 * hh:32 * hh + 32], (64, 32)))
        dtb = np.ascontiguousarray(np.broadcast_to(np.asarray(inp["dn_dt_bias"][0])[None, 32 * hh:32 * hh + 32], (64, 32)))
        per_hh[hh] = {"wqkvz": wqkvz, "wba": wba, "convw": convw, "alog": alog.astype(np.float32),
                      "dtb": dtb.astype(np.float32)}
    xts = {}
    for c in range(8):
        b, hh = c // 2, c % 2
        if b not in xts:
            xts[b] = _fm(_seq_pad(inp, b))
        m = dict(shared)
        m.update(per_hh[hh])
        m["xT"] = xts[b]
        maps.append(m)
    return maps


def stageA_gather(outs):
    og = np.zeros((4, LPAD, 8192), ml_dtypes.bfloat16)
    for c in range(8):
        b, hh = c // 2, c % 2
        o = np.asarray(outs[c])
        og[b, :, hh * 4096:(hh + 1) * 4096] = o.transpose(2, 1, 0).reshape(LPAD, 4096)
    return og


_NC_CACHE = {}


def kernel(**inputs):
    inp = {k: np.asarray(v) for k, v in inputs.items()}
    if "A" not in _NC_CACHE:
        _NC_CACHE["A"] = build_stageA()
        _NC_CACHE["B"] = build_stageB()
    resA = run_bass_kernel_spmd(_NC_CACHE["A"], stageA_inputs(inp), core_ids=list(range(8)))
    og = stageA_gather([r["oT"] for r in resA.results])
    resB = run_bass_kernel_spmd(_NC_CACHE["B"], stageB_inputs(inp, og), core_ids=list(range(8)))
    return stageB_gather([r["outT"] for r in resB.results])
```
